# Optimizing a Trainium2 kernel written in Bass

```python
import math
import jax
import jax.numpy as jnp
from jax import lax
import numpy as np

D_MODEL = 1024
BATCH = 2
SEQ = 8192
DEPTH = 4
DEC_BATCH = 1
DEC_SEQ = 16384
PAST_LEN = 128

N_AB_LAYERS = (DEPTH + 1) // 2
N_CD_LAYERS = DEPTH // 2
D_FF = 4 * D_MODEL
MIX_WIDTH = 3 * D_MODEL // 2
NORM_EPS = 1e-5

RWKV_HEAD = 64
RWKV_WIDTH = D_MODEL // 2
RWKV_HEADS = RWKV_WIDTH // RWKV_HEAD
DECAY_LORA = 64
ICLR_LORA = 64
GATE_LORA = 128
RWKV_GN_EPS = 64e-5
A_IN = 3 * RWKV_WIDTH + 2 * DECAY_LORA + 2 * ICLR_LORA + GATE_LORA

SSM_HEAD = 64
SSM_WIDTH = D_MODEL
SSM_HEADS = SSM_WIDTH // SSM_HEAD
SSM_GROUPS = 2
SSM_STATE = 128
SSM_CONV = 4
SSM_CHUNK = 128
SSM_CONV_DIM = SSM_WIDTH + 2 * SSM_GROUPS * SSM_STATE
B_IN = SSM_WIDTH + SSM_CONV_DIM + 2 * SSM_HEADS
AB_IN = A_IN + B_IN

HG_HEADS = 8
HG_KDIM = 128
HG_VDIM = 64
HG_FDIM = HG_HEADS * HG_KDIM
HG_WIDTH = HG_HEADS * HG_VDIM
HG_CHUNK = 16
C_IN = 3 * HG_FDIM + 2 * HG_WIDTH

RET_HEADS = 4
RET_KDIM = 128
RET_VDIM = 256
RET_QK = RET_HEADS * RET_KDIM
RET_WIDTH = RET_HEADS * RET_VDIM
RET_CHUNK = 128
ROPE_BASE = 10000.0
D_IN = 2 * RET_QK + 2 * RET_WIDTH
CD_IN = C_IN + D_IN

kernel_name = "hybrid_bidir_rwkv7_mamba2_hgrn2_retnet_encoder"


def rms_norm(x, g, eps=NORM_EPS):
    xf = x.astype(jnp.float32)
    y = xf * lax.rsqrt(jnp.mean(xf * xf, axis=-1, keepdims=True) + eps)
    return (y * g.astype(jnp.float32)).astype(x.dtype)


def head_rms_norm(x, w, eps):
    xf = x.astype(jnp.float32)
    y = xf * lax.rsqrt(jnp.mean(xf * xf, axis=-1, keepdims=True) + eps)
    y = y.reshape(*x.shape[:-2], -1)
    return (y * w.astype(jnp.float32)).astype(x.dtype)


def head_group_norm(x, w, b, eps):
    xf = x.astype(jnp.float32)
    xc = xf - jnp.mean(xf, axis=-1, keepdims=True)
    y = xc * lax.rsqrt(jnp.mean(xc * xc, axis=-1, keepdims=True) + eps)
    y = y.reshape(*x.shape[:-2], -1)
    return (y * w.astype(jnp.float32) + b.astype(jnp.float32)).astype(x.dtype)


def l2_normalize(x, eps=1e-12):
    xf = x.astype(jnp.float32)
    n = jnp.sqrt(jnp.sum(xf * xf, axis=-1, keepdims=True))
    return (xf / jnp.maximum(n, eps)).astype(x.dtype)


def split_heads(t, n_heads, head_dim):
    return t.reshape(*t.shape[:-1], n_heads, head_dim)


def both_dirs(fwd, bwd):
    return jnp.concatenate([fwd, jnp.flip(bwd, axis=1)], axis=0)


def split_dirs(t):
    return both_dirs(t[:, :, 0], t[:, :, 1])


def merge_dirs(y):
    n = y.shape[0] // 2
    return y[:n] + jnp.flip(y[n:], axis=1)


def shift_prev(x):
    return jnp.pad(x, ((0, 0), (1, 0), (0, 0)))[:, :-1]


def shift_next(x):
    return jnp.pad(x, ((0, 0), (0, 1), (0, 0)))[:, 1:]


def centred_depthwise_conv(x, w, b):
    width = w.shape[0]
    left = width // 2
    T = x.shape[1]
    xp = jnp.pad(x, ((0, 0), (left, width - 1 - left), (0, 0)))
    y = b
    for j in range(width):
        y = y + xp[:, j:j + T] * w[j]
    return y


def rotary(x):
    T, half = x.shape[1], x.shape[-1] // 2
    inv_freq = ROPE_BASE ** (-jnp.arange(half, dtype=jnp.float32) / half)
    ang = jnp.arange(T, dtype=jnp.float32)[:, None] * inv_freq[None, :]
    cos = jnp.cos(ang)[:, None, :]
    sin = jnp.sin(ang)[:, None, :]
    xf = x.astype(jnp.float32)
    x1, x2 = xf[..., :half], xf[..., half:]
    return jnp.concatenate([x1 * cos - x2 * sin, x2 * cos + x1 * sin], axis=-1).astype(x.dtype)


def hgrn_lower_bounds(lb_param):
    p = jax.nn.softmax(lb_param.astype(jnp.float32), axis=1)
    return jnp.cumsum(p, axis=1) - p[:, :1]


def rwkv7_scan(r, w, k, v, kk, a):
    n, _, h, dh = r.shape

    def step(S, inp):
        r_t, w_t, k_t, v_t, kk_t, a_t = (t.astype(jnp.float32) for t in inp)
        sa = jnp.einsum("nhij,nhj->nhi", S, -kk_t)
        S = (S * w_t[:, :, None, :] + sa[:, :, :, None] * (kk_t * a_t)[:, :, None, :]
             + v_t[:, :, :, None] * k_t[:, :, None, :])
        return S, jnp.einsum("nhij,nhj->nhi", S, r_t)

    xs = tuple(jnp.moveaxis(t, 1, 0) for t in (r, w, k, v, kk, a))
    _, o = lax.scan(step, jnp.zeros((n, h, dh, dh), jnp.float32), xs)
    return jnp.moveaxis(o, 0, 1)


def ssd_chunk_scan(x, da, bm, cm):
    n, T, H, P = x.shape
    G, S = bm.shape[2], bm.shape[3]
    E, L = H // G, SSM_CHUNK
    nc = T // L
    x = x.reshape(n, nc, L, G, E, P)
    bm = bm.reshape(n, nc, L, G, S)
    cm = cm.reshape(n, nc, L, G, S)
    a_cs = jnp.cumsum(da.astype(jnp.float32).reshape(n, nc, L, G, E).transpose(0, 1, 3, 4, 2), axis=-1)
    causal = jnp.tril(jnp.ones((L, L), dtype=bool))
    seg = jnp.where(causal, a_cs[..., :, None] - a_cs[..., None, :], -jnp.inf)
    cb = jnp.einsum("nclgs,ncmgs->ncglm", cm, bm)
    y_diag = jnp.einsum("ncgelm,ncmgep->nclgep", cb[:, :, :, None] * jnp.exp(seg), x)
    states = jnp.einsum("nclgs,ncgel,nclgep->ncgeps", bm, jnp.exp(a_cs[..., -1:] - a_cs), x)
    chunk_decay = jnp.exp(a_cs[..., -1])

    def step(h, inp):
        st, dec = inp
        return h * dec[..., None, None] + st.astype(jnp.float32), h

    _, h_prev = lax.scan(step, jnp.zeros((n, G, E, P, S), jnp.float32),
                         (jnp.moveaxis(states, 1, 0), jnp.moveaxis(chunk_decay, 1, 0)))
    y_off = jnp.einsum("nclgs,ncgeps,ncgel->nclgep", cm, jnp.moveaxis(h_prev, 0, 1), jnp.exp(a_cs))
    return (y_diag + y_off).reshape(n, T, H, P)


def gla_chunk_scan(q, k, v, log_f):
    n, T, H, K = q.shape
    V, L = v.shape[-1], HG_CHUNK
    nc = T // L
    q = q.astype(jnp.float32).reshape(n, nc, L, H, K)
    k = k.astype(jnp.float32).reshape(n, nc, L, H, K)
    v = v.reshape(n, nc, L, H, V)
    b = jnp.cumsum(log_f.astype(jnp.float32).reshape(n, nc, L, H, K), axis=2)
    b_ref = b[:, :, L // 2:L // 2 + 1]
    causal = jnp.tril(jnp.ones((L, L), dtype=bool))
    attn = jnp.einsum("nclhk,ncmhk->nchlm", q * jnp.exp(b - b_ref), k * jnp.exp(b_ref - b))
    o_intra = jnp.einsum("nchlm,ncmhv->nclhv", jnp.where(causal, attn, 0.0), v)

    def step(S, inp):
        qc, kc, vc, dec = inp
        o = jnp.einsum("nlhk,nhkv->nlhv", qc, S)
        S = S * dec[..., None] + jnp.einsum("nlhk,nlhv->nhkv", kc, vc.astype(jnp.float32))
        return S, o

    xs = (jnp.moveaxis(q * jnp.exp(b), 1, 0),
          jnp.moveaxis(k * jnp.exp(b[:, :, -1:] - b), 1, 0),
          jnp.moveaxis(v, 1, 0),
          jnp.moveaxis(jnp.exp(b[:, :, -1]), 1, 0))
    _, o_inter = lax.scan(step, jnp.zeros((n, H, K, V), jnp.float32), xs)
    return (o_intra + jnp.moveaxis(o_inter, 0, 1)).reshape(n, T, H, V)


def retention_chunk_scan(q, k, v, log_gamma):
    n, T, H, K = q.shape
    V, L = v.shape[-1], RET_CHUNK
    nc = T // L
    q = q.astype(jnp.float32).reshape(n, nc, L, H, K)
    k = k.astype(jnp.float32).reshape(n, nc, L, H, K)
    v = v.reshape(n, nc, L, H, V)
    pos = jnp.arange(L, dtype=jnp.float32)
    rel = pos[:, None] - pos[None, :]
    intra_decay = jnp.exp(jnp.where(rel >= 0, rel * log_gamma[:, :, None, None], -jnp.inf))
    scores = jnp.einsum("nclhk,ncmhk->nchlm", q, k) * intra_decay[:, None]
    o_intra = jnp.einsum("nchlm,ncmhv->nclhv", scores, v)
    q_dec = q * jnp.exp((pos[:, None] + 1.0) * log_gamma[:, None, :])[:, None, :, :, None]
    k_dec = k * jnp.exp((L - 1.0 - pos)[:, None] * log_gamma[:, None, :])[:, None, :, :, None]
    chunk_decay = jnp.exp(L * log_gamma)

    def step(S, inp):
        qc, kc, vc = inp
        o = jnp.einsum("nlhk,nhkv->nlhv", qc, S)
        S = S * chunk_decay[:, :, None, None] + jnp.einsum("nlhk,nlhv->nhkv", kc, vc.astype(jnp.float32))
        return S, o

    xs = (jnp.moveaxis(q_dec, 1, 0), jnp.moveaxis(k_dec, 1, 0), jnp.moveaxis(v, 1, 0))
    _, o_inter = lax.scan(step, jnp.zeros((n, H, K, V), jnp.float32), xs)
    return (o_intra + jnp.moveaxis(o_inter, 0, 1)).reshape(n, T, H, V)


def rwkv7_mixer(u, mu, w0, w2, a0, a2, g2, k_k, k_a, r_k, gn_w, gn_b):
    bsz, T, _ = u.shape
    W = RWKV_WIDTH
    u = u + mu[0] * (shift_prev(u) - u) + mu[1] * (shift_next(u) - u)
    o1 = 3 * W
    o2 = o1 + 2 * DECAY_LORA
    o3 = o2 + 2 * ICLR_LORA
    r, k, v = u[..., :W], u[..., W:2 * W], u[..., 2 * W:o1]
    w_lo = u[..., o1:o2].reshape(bsz, T, 2, DECAY_LORA)
    a_lo = u[..., o2:o3].reshape(bsz, T, 2, ICLR_LORA)
    g = jax.nn.sigmoid(u[..., o3:]) @ g2
    w_raw = (w0 + jnp.einsum("btdr,drc->btdc", jnp.tanh(w_lo), w2)).astype(jnp.float32)
    decay = jnp.exp(-jnp.exp(-jax.nn.softplus(-w_raw) - 0.5))
    a = jax.nn.sigmoid(a0 + jnp.einsum("btdr,drc->btdc", a_lo, a2))
    kk = l2_normalize(split_heads(k * k_k, RWKV_HEADS, RWKV_HEAD))
    k_dir = k[:, :, None, :] * (1.0 + (a - 1.0) * k_a)
    rh = split_heads(r, RWKV_HEADS, RWKV_HEAD)
    vh = split_heads(v, RWKV_HEADS, RWKV_HEAD)
    o = rwkv7_scan(both_dirs(rh, rh),
                   split_dirs(split_heads(decay, RWKV_HEADS, RWKV_HEAD)),
                   split_dirs(split_heads(k_dir, RWKV_HEADS, RWKV_HEAD)),
                   both_dirs(vh, vh), both_dirs(kk, kk),
                   split_dirs(split_heads(a, RWKV_HEADS, RWKV_HEAD)))
    o = head_group_norm(merge_dirs(o), gn_w, gn_b, RWKV_GN_EPS)
    bonus = jnp.sum(rh * split_heads(k, RWKV_HEADS, RWKV_HEAD) * split_heads(r_k, RWKV_HEADS, RWKV_HEAD),
                    axis=-1, keepdims=True) * vh
    return ((o + bonus.reshape(bsz, T, W).astype(o.dtype)) * g).astype(u.dtype)


def mamba2_mixer(u, conv_w, conv_b, dt_bias, a_log, d_skip, norm_w):
    bsz, T, _ = u.shape
    gs = SSM_GROUPS * SSM_STATE
    z = u[..., :SSM_WIDTH]
    xbc = jax.nn.silu(centred_depthwise_conv(u[..., SSM_WIDTH:SSM_WIDTH + SSM_CONV_DIM], conv_w, conv_b))
    dt = u[..., SSM_WIDTH + SSM_CONV_DIM:].reshape(bsz, T, 2, SSM_HEADS)
    xs = xbc[..., :SSM_WIDTH].reshape(bsz, T, SSM_HEADS, SSM_HEAD)
    bm = xbc[..., SSM_WIDTH:SSM_WIDTH + gs].reshape(bsz, T, SSM_GROUPS, SSM_STATE)
    cm = xbc[..., SSM_WIDTH + gs:].reshape(bsz, T, SSM_GROUPS, SSM_STATE)
    dt = jax.nn.softplus(dt.astype(jnp.float32) + dt_bias.astype(jnp.float32))
    da = -dt * jnp.exp(a_log.astype(jnp.float32))
    y = ssd_chunk_scan(split_dirs(xs[:, :, None] * dt[..., None]), split_dirs(da),
                       both_dirs(bm, bm), both_dirs(cm, cm))
    y = merge_dirs(y) + xs * d_skip[:, None]
    y = y.reshape(bsz, T, SSM_WIDTH) * jax.nn.silu(z)
    return head_rms_norm(y.reshape(bsz, T, SSM_GROUPS, -1), norm_w, 1e-5).astype(u.dtype)


def hgrn2_mixer(u, lower_bound, norm_w):
    bsz, T, _ = u.shape
    F = HG_FDIM
    q = split_heads(u[..., :F], HG_HEADS, HG_KDIM)
    f_logit = u[..., F:3 * F].astype(jnp.float32).reshape(bsz, T, 2, F)
    f = lower_bound + (1.0 - lower_bound) * jax.nn.sigmoid(f_logit)
    i = split_heads(u[..., 3 * F:3 * F + HG_WIDTH], HG_HEADS, HG_VDIM)
    g = u[..., 3 * F + HG_WIDTH:]
    fh = split_heads(f, HG_HEADS, HG_KDIM)
    o = gla_chunk_scan(both_dirs(q, q), split_dirs(1.0 - fh), both_dirs(i, i), split_dirs(jnp.log(fh)))
    o = head_rms_norm(merge_dirs(o), norm_w, 1e-5)
    return (o * jax.nn.sigmoid(g)).astype(u.dtype)


def retention_mixer(u, log_decay, gn_w, gn_b):
    bsz, T, _ = u.shape
    q = rotary(split_heads(u[..., :RET_QK], RET_HEADS, RET_KDIM))
    k = rotary(split_heads(u[..., RET_QK:2 * RET_QK], RET_HEADS, RET_KDIM)) * RET_KDIM ** -0.5
    v = split_heads(u[..., 2 * RET_QK:2 * RET_QK + RET_WIDTH], RET_HEADS, RET_VDIM)
    g = u[..., 2 * RET_QK + RET_WIDTH:]
    log_gamma = jnp.repeat(-jnp.exp(log_decay.astype(jnp.float32)), bsz, axis=0)
    o = retention_chunk_scan(both_dirs(q, q), both_dirs(k, k), both_dirs(v, v), log_gamma)
    o = head_group_norm(merge_dirs(o), gn_w, gn_b, 1e-5)
    return (o * jax.nn.silu(g)).astype(u.dtype)


def setup_inputs(seed: int = 0) -> dict:
    key = jax.random.key(seed)
    keys = jax.random.split(key, 32)

    def nrm(i, shape, scale):
        return scale * jax.random.normal(keys[i], shape, jnp.float32)

    def unif(i, shape, lo, hi):
        return jax.random.uniform(keys[i], shape, jnp.float32, lo, hi)

    na, nc, W = N_AB_LAYERS, N_CD_LAYERS, RWKV_WIDTH
    dt0 = jnp.exp(unif(22, (na, 2, SSM_HEADS), math.log(1e-3), math.log(1e-1)))
    ret_base = jnp.log(-jnp.log(1.0 - 2.0 ** (-5.0 - jnp.arange(RET_HEADS, dtype=jnp.float32))))
    return {
        "x_prompt": nrm(0, (BATCH, SEQ, D_MODEL), 1.0),
        "x_sample": nrm(1, (DEC_BATCH, DEC_SEQ, D_MODEL), 1.0),
        "ln_mix": 1.0 + nrm(2, (DEPTH, D_MODEL), 0.02),
        "ln_ffn": 1.0 + nrm(3, (DEPTH, D_MODEL), 0.02),
        "ln_final": 1.0 + nrm(4, (D_MODEL,), 0.02),
        "w_out": nrm(5, (DEPTH, MIX_WIDTH, D_MODEL), MIX_WIDTH ** -0.5),
        "ffn_w1": nrm(6, (DEPTH, D_MODEL, D_FF), D_MODEL ** -0.5),
        "ffn_w2": nrm(7, (DEPTH, D_FF, D_MODEL), D_FF ** -0.5),
        "ab_w_in": nrm(8, (na, D_MODEL, AB_IN), D_MODEL ** -0.5),
        "rw_mu": unif(9, (na, 2, A_IN), 0.0, 0.5),
        "rw_w0": unif(10, (na, 2, W), -6.0, 1.0),
        "rw_w2": nrm(11, (na, 2, DECAY_LORA, W), 0.1 * DECAY_LORA ** -0.5),
        "rw_a0": nrm(12, (na, 2, W), 0.1),
        "rw_a2": nrm(13, (na, 2, ICLR_LORA, W), 0.5 * ICLR_LORA ** -0.5),
        "rw_g2": nrm(14, (na, GATE_LORA, W), GATE_LORA ** -0.5),
        "rw_k_k": 0.85 + nrm(15, (na, W), 0.05),
        "rw_k_a": 1.0 + nrm(16, (na, W), 0.05),
        "rw_r_k": nrm(17, (na, W), 0.1),
        "rw_gn_w": 1.0 + nrm(18, (na, W), 0.02),
        "rw_gn_b": nrm(19, (na, W), 0.02),
        "ssm_conv_w": nrm(20, (na, SSM_CONV, SSM_CONV_DIM), SSM_CONV ** -0.5),
        "ssm_conv_b": nrm(21, (na, SSM_CONV_DIM), 0.02),
        "ssm_dt_bias": dt0 + jnp.log(-jnp.expm1(-dt0)),
        "ssm_a_log": jnp.log(unif(23, (na, 2, SSM_HEADS), 1.0, 16.0)),
        "ssm_d": 1.0 + nrm(24, (na, SSM_HEADS), 0.1),
        "ssm_norm_w": 1.0 + nrm(25, (na, SSM_WIDTH), 0.02),
        "cd_w_in": nrm(26, (nc, D_MODEL, CD_IN), D_MODEL ** -0.5),
        "hg_lb": nrm(27, (2, nc, HG_FDIM), 1.0),
        "hg_norm_w": 1.0 + nrm(28, (nc, HG_WIDTH), 0.02),
        "ret_log_decay": ret_base + nrm(29, (nc, 2, RET_HEADS), 0.05),
        "ret_gn_w": 1.0 + nrm(30, (nc, RET_WIDTH), 0.02),
        "ret_gn_b": nrm(31, (nc, RET_WIDTH), 0.02),
    }


def reference(x_prompt, x_sample, ln_mix, ln_ffn, ln_final, w_out, ffn_w1, ffn_w2,
              ab_w_in, rw_mu, rw_w0, rw_w2, rw_a0, rw_a2, rw_g2, rw_k_k, rw_k_a, rw_r_k,
              rw_gn_w, rw_gn_b, ssm_conv_w, ssm_conv_b, ssm_dt_bias, ssm_a_log, ssm_d,
              ssm_norm_w, cd_w_in, hg_lb, hg_norm_w, ret_log_decay, ret_gn_w, ret_gn_b):
    lower_bounds = hgrn_lower_bounds(hg_lb)

    def trunk(h):
        for layer in range(DEPTH):
            j = layer // 2
            hn = rms_norm(h, ln_mix[layer])
            if layer % 2 == 0:
                u = hn @ ab_w_in[j]
                mixed = jnp.concatenate([
                    rwkv7_mixer(u[..., :A_IN], rw_mu[j], rw_w0[j], rw_w2[j], rw_a0[j], rw_a2[j],
                                rw_g2[j], rw_k_k[j], rw_k_a[j], rw_r_k[j], rw_gn_w[j], rw_gn_b[j]),
                    mamba2_mixer(u[..., A_IN:], ssm_conv_w[j], ssm_conv_b[j], ssm_dt_bias[j],
                                 ssm_a_log[j], ssm_d[j], ssm_norm_w[j]),
                ], axis=-1)
            else:
                u = hn @ cd_w_in[j]
                mixed = jnp.concatenate([
                    hgrn2_mixer(u[..., :C_IN], lower_bounds[:, j], hg_norm_w[j]),
                    retention_mixer(u[..., C_IN:], ret_log_decay[j], ret_gn_w[j], ret_gn_b[j]),
                ], axis=-1)
            h = h + mixed @ w_out[layer]
            hn = rms_norm(h, ln_ffn[layer])
            h = h + jnp.square(jax.nn.relu(hn @ ffn_w1[layer])) @ ffn_w2[layer]
        return rms_norm(h, ln_final)

    y_prompt = trunk(x_prompt)
    y_sample = trunk(x_sample)
    return (y_prompt, y_sample)
```

```python
import contextlib
import math
import numpy as np
import ml_dtypes
import concourse.bass as bass
import concourse.mybir as mybir
from concourse.bass_utils import run_bass_kernel_spmd

F32 = mybir.dt.float32
BF16 = mybir.dt.bfloat16
AF = mybir.ActivationFunctionType
ALU = mybir.AluOpType
AX = mybir.AxisListType

N_DSEM = 24
SQ = "act"

D = 1024
DFF = 4096
MIXW = 1536
A_IN = 1920
B_IN = 2592
AB_IN = 4512
C_IN = 4096
D_IN = 3072
CD_IN = 7168
EPS = 1e-5


class Trk:
    __slots__ = ("w", "r")

    def __init__(self):
        self.w = {}
        self.r = {}


class Buf:
    def __init__(self, t, name):
        self.t = t
        self.name = name
        self.whole = Trk()
        self.parts = {}

    def trks(self, key):
        if key is None:
            return [self.whole] + list(self.parts.values())
        if key not in self.parts:
            self.parts[key] = Trk()
        return [self.whole, self.parts[key]]

    def own(self, key):
        if key is None:
            return [self.whole] + list(self.parts.values())
        return [self.parts.setdefault(key, Trk())]

    def __getitem__(self, idx):
        return V(self, self.t[idx], None)

    def k(self, key):
        return V(self, self.t[:], key)


class V:
    __slots__ = ("buf", "ap", "key")

    def __init__(self, buf, ap, key):
        self.buf = buf
        self.ap = ap
        self.key = key

    def __getitem__(self, idx):
        return V(self.buf, self.ap[idx], self.key)

    def re(self, s, **kw):
        return V(self.buf, self.ap.rearrange(s, **kw), self.key)

    def bc(self, shape):
        return V(self.buf, self.ap.to_broadcast(list(shape)), self.key)

    def bitcast(self, dt):
        return V(self.buf, self.ap.bitcast(dt), self.key)

    def pbc(self, n):
        return V(self.buf, self.ap.partition_broadcast(n), self.key)


class Ring:
    def __init__(self, bufs):
        self.bufs = bufs
        self.i = 0

    def next(self):
        b = self.bufs[self.i % len(self.bufs)]
        self.i += 1
        return b


class Ctx:
    def __init__(self, nc):
        self.nc = nc
        self.es = contextlib.ExitStack()
        self.stack = [self.es]
        self.eng = {"pe": nc.tensor, "dve": nc.vector, "act": nc.scalar, "pool": nc.gpsimd, "sp": nc.sync}
        self.sem = {}
        self.cnt = {}
        self.seen = {e: {} for e in self.eng}
        self.semobj = {}
        for e in self.eng:
            s = self.es.enter_context(nc.semaphore("s_" + e))
            self.sem[e] = s
            self.semobj[("e", e)] = s
            self.cnt[e] = 0
        self.dsem = []
        self.dcnt = []
        for i in range(N_DSEM):
            s = self.es.enter_context(nc.semaphore(f"d_{i}"))
            self.dsem.append(s)
            self.semobj[("d", i)] = s
            self.dcnt.append(0)
        self.ccsem = self.es.enter_context(nc.semaphore("cc"))
        self.semobj[("cc", 0)] = self.ccsem
        self.cccnt = 0
        self.dnext = 0
        self.n_inst = 0
        self.uid = 0

    @contextlib.contextmanager
    def scope(self):
        es = contextlib.ExitStack()
        self.stack.append(es)
        try:
            with es:
                yield
                self.barrier()
        finally:
            self.stack.pop()

    def sb(self, name, shape, dtype=F32):
        self.uid += 1
        return Buf(self.stack[-1].enter_context(self.nc.sbuf_tensor(f"{name}_{self.uid}", list(shape), dtype)), name)

    def ring(self, name, n, shape, dtype=F32):
        return Ring([self.sb(f"{name}{i}", shape, dtype) for i in range(n)])

    def ps(self, name, shape, dtype=F32):
        self.uid += 1
        return Buf(self.stack[-1].enter_context(self.nc.psum_tensor(f"{name}_{self.uid}", list(shape), dtype)), name)

    def dram(self, name, shape, dtype=F32, kind="Internal"):
        t = self.nc.dram_tensor(name, list(shape), dtype, kind=kind)
        return Buf(t.ap(), name)

    def _wait(self, e, ev):
        sid, val, src = ev
        if src == e and e == "pe":
            return
        if self.seen[e].get(sid, 0) >= val:
            return
        self.seen[e][sid] = val
        self.eng[e].wait_ge(self.semobj[sid], val)

    def _deps(self, e, reads, writes):
        for v in reads:
            for tr in v.buf.trks(v.key):
                for sid, (val, src) in tr.w.items():
                    self._wait(e, (sid, val, src))
        for v in writes:
            for tr in v.buf.trks(v.key):
                for sid, (val, src) in tr.w.items():
                    self._wait(e, (sid, val, src))
                for sid, (val, src) in tr.r.items():
                    self._wait(e, (sid, val, src))

    def _mark(self, ev, reads, writes):
        sid, val, src = ev
        for v in reads:
            for tr in v.buf.own(v.key):
                tr.r[sid] = (val, src)
        for v in writes:
            for tr in v.buf.own(v.key):
                tr.w[sid] = (val, src)
                tr.r = {}

    def issue(self, e, fn, reads, writes):
        self._deps(e, reads, writes)
        inst = fn()
        self.cnt[e] += 1
        inst.then_inc(self.sem[e], 1)
        ev = (("e", e), self.cnt[e], e)
        self._mark(ev, reads, writes)
        self.n_inst += 1
        return ev

    def barrier(self):
        for e in self.eng:
            for e2 in self.eng:
                if e2 != e and self.cnt[e2] > 0:
                    self._wait(e, (("e", e2), self.cnt[e2], e2))
            for i in range(N_DSEM):
                if self.dcnt[i] > 0:
                    self._wait(e, (("d", i), self.dcnt[i], "dma"))

    def dma(self, q, out, in_, nonc=False):
        i = self.dnext
        self.dnext = (self.dnext + 1) % N_DSEM
        sid = ("d", i)
        if self.dcnt[i] > 0:
            self._wait(q, (sid, self.dcnt[i], "dma"))
        self._deps(q, [in_], [out])
        if nonc:
            self.eng[q].dma_start(out=out.ap, in_=in_.ap, allow_slow_non_contiguous=True).then_inc(self.dsem[i], 16)
        else:
            self.eng[q].dma_start(out=out.ap, in_=in_.ap).then_inc(self.dsem[i], 16)
        self.dcnt[i] += 16
        ev = (sid, self.dcnt[i], "dma")
        self._mark(ev, [in_], [out])
        self.n_inst += 1
        return ev

    def allgather(self, out, in_, n_cores):
        self._deps("pool", [in_], [out])
        self.nc.gpsimd.collective_compute("AllGather", ALU.bypass, replica_groups=[list(range(n_cores))],
                                          ins=[in_.ap.opt()], outs=[out.ap.opt()]).then_inc(self.ccsem)
        self.cccnt += 1
        ev = (("cc", 0), self.cccnt, "cc")
        self._mark(ev, [in_], [out])
        self._wait("pool", ev)
        return ev

    def finish(self):
        self.barrier()

    def mm(self, out, lhsT, rhs, start=True, stop=True):
        return self.issue("pe", lambda: self.nc.tensor.matmul(out.ap, lhsT=lhsT.ap, rhs=rhs.ap, start=start, stop=stop),
                          [lhsT, rhs] + ([] if start else [out]), [out])

    def tr(self, out, in_, ident):
        return self.issue("pe", lambda: self.nc.tensor.transpose(out.ap, in_.ap, ident.ap), [in_, ident], [out])

    def act(self, out, in_, func, bias=None, scale=1.0, accum=None):
        reads = [in_]
        kw = {}
        if bias is not None:
            if isinstance(bias, V):
                reads.append(bias)
                kw["bias"] = bias.ap
            else:
                kw["bias"] = bias
        if isinstance(scale, V):
            reads.append(scale)
            kw["scale"] = scale.ap
        elif scale != 1.0:
            kw["scale"] = scale
        writes = [out]
        if accum is not None:
            writes.append(accum)
            kw["accum_out"] = accum.ap
        return self.issue("act", lambda: self.nc.scalar.activation(out=out.ap, in_=in_.ap, func=func, **kw), reads, writes)

    def tt(self, e, out, in0, in1, op):
        return self.issue(e, lambda: self.eng[e].tensor_tensor(out=out.ap, in0=in0.ap, in1=in1.ap, op=op), [in0, in1], [out])

    def ts(self, e, out, in0, s1, op0, s2=None, op1=None):
        reads = [in0]
        a1 = s1
        if isinstance(s1, V):
            reads.append(s1)
            a1 = s1.ap
        a2 = s2
        if isinstance(s2, V):
            reads.append(s2)
            a2 = s2.ap
        if op1 is None:
            return self.issue(e, lambda: self.eng[e].tensor_scalar(out=out.ap, in0=in0.ap, scalar1=a1, scalar2=None, op0=op0), reads, [out])
        return self.issue(e, lambda: self.eng[e].tensor_scalar(out=out.ap, in0=in0.ap, scalar1=a1, scalar2=a2, op0=op0, op1=op1), reads, [out])

    def stt(self, out, in0, scalar, in1, op0, op1):
        reads = [in0, in1]
        a = scalar
        if isinstance(scalar, V):
            reads.append(scalar)
            a = scalar.ap
        return self.issue("dve", lambda: self.nc.vector.scalar_tensor_tensor(out=out.ap, in0=in0.ap, scalar=a, in1=in1.ap, op0=op0, op1=op1), reads, [out])

    def copy(self, e, out, in_):
        if e == "act":
            return self.issue("act", lambda: self.nc.scalar.copy(out=out.ap, in_=in_.ap), [in_], [out])
        return self.issue(e, lambda: self.eng[e].tensor_copy(out=out.ap, in_=in_.ap), [in_], [out])

    def memset(self, e, out, val):
        return self.issue(e, lambda: self.eng[e].memset(out.ap, val), [], [out])

    def scan(self, out, d0, d1, initial, op0, op1):
        reads = [d0, d1]
        a = initial
        if isinstance(initial, V):
            reads.append(initial)
            a = initial.ap
        return self.issue("dve", lambda: self.nc.vector.tensor_tensor_scan(out=out.ap, data0=d0.ap, data1=d1.ap, initial=a, op0=op0, op1=op1), reads, [out])

    def recip(self, out, in_):
        return self.issue("dve", lambda: self.nc.vector.reciprocal(out=out.ap, in_=in_.ap), [in_], [out])

    def reduce(self, out, in_, op=ALU.add, axis=AX.X):
        def fn():
            with self.nc.allow_low_precision(reason="fp32 internal accumulation, output cast only"):
                return self.nc.vector.tensor_reduce(out=out.ap, in_=in_.ap, axis=axis, op=op)
        return self.issue("dve", fn, [in_], [out])


CST_NAMES = ["ident", "blk64", "maskI_f", "maskI_b", "maskS_f", "maskS_b", "m64_f", "m64_b",
             "rel_f", "rel_b", "posq_f", "posq_b", "neg_f", "neg_b", "swap", "ones", "bd16", "off16", "off32", "off64",
             "gl16_f", "gl32_f", "gl64_f", "gl128_f", "gl16_b", "gl32_b", "gl64_b", "gl128_b"]
NCST = len(CST_NAMES)


def make_consts():
    p = np.arange(128)
    s = p[:, None]
    t = p[None, :]
    c = {}
    c["ident"] = (s == t)
    c["blk64"] = (s // 64 == t // 64)
    c["maskI_f"] = (s <= t)
    c["maskI_b"] = (s >= t)
    c["maskS_f"] = (s < t)
    c["maskS_b"] = (s > t)
    c["m64_f"] = (s <= t) & (s // 64 == t // 64)
    c["m64_b"] = (s >= t) & (s // 64 == t // 64)
    big = 1.0e5
    c["rel_f"] = np.where(s <= t, (t - s).astype(np.float64), big)
    c["rel_b"] = np.where(s >= t, (s - t).astype(np.float64), big)
    c["posq_f"] = np.broadcast_to(t + 1.0, (128, 128))
    c["posq_b"] = np.broadcast_to(128.0 - t, (128, 128))
    c["neg_f"] = np.where(s <= t, 0.0, -30000.0)
    c["neg_b"] = np.where(s >= t, 0.0, -30000.0)
    c["swap"] = (s == (t + 64) % 128)
    c["ones"] = np.ones((128, 128))
    c["bd16"] = (s // 16 == t // 16)
    for b in (16, 32, 64):
        c[f"off{b}"] = (s // (2 * b) == t // (2 * b)) & (s // b != t // b)
    c["gl16_f"] = (s // 16 == t // 16) & (s <= t)
    c["gl16_b"] = (s // 16 == t // 16) & (s >= t)
    for b in (32, 64, 128):
        same = (s // b == t // b)
        c[f"gl{b}_f"] = same & ((s % b) < b // 2) & ((t % b) >= b // 2)
        c[f"gl{b}_b"] = same & ((s % b) >= b // 2) & ((t % b) < b // 2)
    arr = np.concatenate([np.asarray(c[n], dtype=np.float32) for n in CST_NAMES], axis=1)
    posk = np.stack([127.0 - p, p.astype(np.float64)], axis=1).astype(np.float32)
    return np.ascontiguousarray(np.concatenate([arr, posk], axis=1).astype(np.float32))


class Prog:
    def __init__(self, cfg):
        self.cfg = cfg
        self.TC = cfg["TC"]
        self.NC = cfg["n_cores"]
        self.layers = cfg["layers"]
        self.NT = 512 if self.TC >= 512 else self.TC
        self.NG = self.TC // self.NT
        self.NCH = self.TC // 128
        self.nc = bass.Bass("TRN2", target_bir_lowering=False)
        self.c = Ctx(self.nc)
        self.shapes = {}
        self.inp = LazyInputs(self)

    def din(self, name, shape, dtype=F32):
        self.shapes[name] = (shape, dtype)


class LazyInputs(dict):
    def __init__(self, prog):
        super().__init__()
        self.prog = prog

    def __missing__(self, name):
        shape, dtype = self.prog.shapes[name]
        b = self.prog.c.dram(name, shape, dtype, kind="ExternalInput")
        self[name] = b
        return b


class Prog(Prog):
    pass

    def declare(self):
        c, TC = self.c, self.TC
        nab = max([j for t, j in self.layers if t == "AB"] + [-1]) + 1
        ncd = max([j for t, j in self.layers if t == "CD"] + [-1]) + 1
        self.nab, self.ncd = max(nab, 1), max(ncd, 1)
        nab, ncd = self.nab, self.ncd
        depth = len(self.layers)
        self.depth = depth
        d = self.din
        d("x", [TC, D])
        d("ln_mix", [depth, D]); d("ln_ffn", [depth, D]); d("ln_final", [D])
        d("w_out", [depth, MIXW, D]); d("ffn_w1", [depth, D, DFF]); d("ffn_w2", [depth, DFF, D])
        d("ab_w_in", [nab, D, AB_IN])
        d("rw_mu", [nab, 2, A_IN]); d("rw_w0", [nab, 2, 512]); d("rw_w2", [nab, 2, 64, 512])
        d("rw_a0", [nab, 2, 512]); d("rw_a2", [nab, 2, 64, 512]); d("rw_g2", [nab, 128, 512])
        d("rw_k_k", [nab, 512]); d("rw_k_a", [nab, 512]); d("rw_r_k", [nab, 512])
        d("rw_gn_w", [nab, 512]); d("rw_gn_b", [nab, 512])
        d("ssm_conv_w", [nab, 4, 1536]); d("ssm_conv_b", [nab, 1536])
        d("ssm_dt_bias", [nab, 2, 16]); d("ssm_a_log", [nab, 2, 16]); d("ssm_d", [nab, 16]); d("ssm_norm_w", [nab, 1024])
        d("cd_w_in", [ncd, D, CD_IN])
        d("hg_lb", [2, ncd, 1024]); d("hg_norm_w", [ncd, 512])
        d("ret_log_decay", [ncd, 2, 4]); d("ret_gn_w", [ncd, 1024]); d("ret_gn_b", [ncd, 1024])
        d("cst", [128, NCST * 128 + 2])
        d("rot_cos", [128, TC]); d("rot_sin", [128, TC])
        d("coef", [1, 64])
        self.y = c.dram("y", [TC, D], F32, kind="ExternalOutput")
        self.hT = c.dram("hT", [D, TC], F32)
        self.hnT = c.dram("hnT", [D, TC], BF16)
        self.mixT = c.dram("mixT", [MIXW, TC], BF16)
        self.aT = c.dram("aT", [DFF, TC], BF16)
        self.uT = c.dram("uT", [CD_IN, TC + 3], F32)

    def consts(self):
        c = self.c
        self.cf = c.sb("cstf", [128, NCST * 128 + 2], F32)
        c.dma("sp", self.cf[:], self.inp["cst"][:])
        self.cb = c.sb("cstb", [128, 16 * 128], BF16)
        c.copy("dve", self.cb[:], self.cf[:, 0:16 * 128])
        self.pq = [c.ps(f"pq{i}", [128, 1024], F32) for i in range(4)]
        self.pb = [self.pq[i // 2].k(i % 2)[:, (i % 2) * 512:(i % 2 + 1) * 512] for i in range(8)]
        self._coef = None
        if self.NC > 1:
            self._coef = c.sb("coef", [128, 64], F32)
            c.dma("sp", self._coef[:], self.inp["coef"][:, :].pbc(128))
        self.rm128 = c.sb("rm128", [128, 512], F32)
        self.rm64 = c.sb("rm64", [128, 512], F32)
        c.memset("pool", self.rm128[:], 1.0)
        c.memset("pool", self.rm64[:], 1.0)
        c.memset("pool", self.rm128[:].re("p (c l) -> p c l", l=128)[:, :, 0:1], 0.0)
        c.memset("pool", self.rm64[:].re("p (c l) -> p c l", l=64)[:, :, 0:1], 0.0)

    def cF(self, name):
        i = CST_NAMES.index(name)
        return self.cf[:, i * 128:(i + 1) * 128]

    def cB(self, name):
        i = CST_NAMES.index(name)
        return self.cb[:, i * 128:(i + 1) * 128]

    def load_gamma(self, src_row):
        g = self.c.sb("gam", [128, 8], F32)
        self.c.dma("sp", g[:], src_row.re("(c p) -> p c", p=128), nonc=True)
        return g

    def epilogue(self, hg, g, gam, dst, N, final=False, store_h=True):
        c = self.c
        col = slice(g * self.NT, g * self.NT + N)
        if store_h:
            c.dma(SQ, self.hT[:, col].re("(c p) n -> p c n", p=128), hg[:, :, 0:N])
        sq = self.ep_sq.next()
        c.act(sq[:, :, 0:N], hg[:, :, 0:N], AF.Square)
        ps = self.pb[7]
        for k in range(8):
            c.mm(ps[:, 0:N], self.cB("ones"), sq[:, k, 0:N], start=(k == 0), stop=(k == 7))
        rs = self.ep_rs.next()
        c.ts("dve", rs[:, 0:N], ps[:, 0:N], 1.0 / D, ALU.mult, EPS, ALU.add)
        c.act(rs[:, 0:N], rs[:, 0:N], AF.Sqrt)
        c.recip(rs[:, 0:N], rs[:, 0:N])
        if final:
            hn = self.ep_hf.next()
        elif isinstance(dst, Buf) and dst is not self.hnT:
            hn = None
        else:
            hn = self.ep_hn.next()
        for k in range(8):
            if hn is None:
                o = dst.k(g)[:, k, col]
            else:
                o = hn[:, k, 0:N]
            c.stt(o, hg[:, k, 0:N], gam[:, k:k + 1], rs[:, 0:N], ALU.mult, ALU.mult)
        if final:
            for tt in range(N // 128):
                yt = self.ep_y.next()
                for half in range(2):
                    ps2 = self.pb[5 + half]
                    for kk in range(4):
                        k = half * 4 + kk
                        c.tr(ps2[:, kk * 128:(kk + 1) * 128], hn[:, k, tt * 128:(tt + 1) * 128], self.cF("ident"))
                    c.copy("act" if half == 0 else "dve", yt[:, half * 512:(half + 1) * 512], ps2[:, :])
                r0 = g * self.NT + tt * 128
                c.dma(SQ, self.y[r0:r0 + 128, :], yt[:])
        elif hn is not None:
            c.dma(SQ, self.hnT[:, col].re("(c p) n -> p c n", p=128), hn[:, :, 0:N])

    def phase0(self):
        c, NT = self.c, self.NT
        with c.scope():
            self.ep_alloc()
            gam = self.load_gamma(self.inp["ln_mix"][0, :])
            xr = c.ring("xt", 2, [128, D], F32)
            hr = c.ring("hg", 2, [128, 8, NT], F32)
            for g in range(self.NG):
                hg = hr.next()
                for tt in range(NT // 128):
                    xt = xr.next()
                    r0 = g * NT + tt * 128
                    c.dma("sp", xt[:], self.inp["x"][r0:r0 + 128, :])
                    for half in range(2):
                        ps = self.pb[half]
                        for kk in range(4):
                            k = half * 4 + kk
                            c.tr(ps[:, kk * 128:(kk + 1) * 128], xt[:, k * 128:(k + 1) * 128], self.cF("ident"))
                        c.copy("act" if half == 0 else "dve", hg[:, half * 4:half * 4 + 4, tt * 128:(tt + 1) * 128],
                               ps[:, :].re("p (k n) -> p k n", k=4))
                self.epilogue(hg, g, gam, self.hnT, NT)

    def ep_alloc(self, final=False):
        c, NT = self.c, self.NT
        self.ep_sq = c.ring("epsq", 1, [128, 8, NT], BF16)
        self.ep_rs = c.ring("eprs", 2, [128, NT], F32)
        if not final:
            self.ep_hn = c.ring("ephn", 2, [128, 8, NT], BF16)
        self.eps_t = c.sb("epst", [128, 1], F32)
        c.memset("pool", self.eps_t[:], EPS)
        if final:
            self.ep_hf = c.ring("ephf", 1, [128, 8, NT], F32)
            self.ep_y = c.ring("epy", 2, [128, D], F32)

    def load_w_block(self, wsrc, c0, w, ring_f, ring_b, kchunks=8):
        c = self.c
        wf = ring_f.next()
        c.dma("sp", wf[:, :, 0:w], wsrc[:, c0:c0 + w].re("(k p) n -> p k n", p=128))
        wb = ring_b.next()
        c.copy("act", wb[:, :, 0:w], wf[:, :, 0:w])
        return wb

    def proj_fm(self, hn, wsrc, c0, ncols, dst_rows, evac_rr):
        c, NT = self.c, self.NT
        done = 0
        while done < ncols:
            w = min(512, ncols - done)
            wb = self.load_w_block(wsrc, c0 + done, w, self.wf_ring, self.wb_ring)
            for sub in range(0, w, 128):
                m = min(128, w - sub)
                if getattr(self, "halo_hn", None) is not None:
                    self.proj_halo(wb, sub, m, dst_rows + done + sub)
                for g in range(self.NG):
                    ps = self.pb[self.pbi % 4]
                    self.pbi += 1
                    for k in range(8):
                        c.mm(ps[0:m, 0:NT], wb[:, k, sub:sub + m], hn.k(g)[:, k, g * NT:(g + 1) * NT], start=(k == 0), stop=(k == 7))
                    st = self.st_ring.next()
                    e = "act" if (self.pbi % 2 == 0) else "dve"
                    c.copy(e, st[0:m, 0:NT], ps[0:m, 0:NT])
                    r0 = dst_rows + done + sub
                    c.dma(SQ, self.uT[r0:r0 + m, 2 + g * NT:2 + (g + 1) * NT], st[0:m, 0:NT])
            done += w

    def proj_tm(self, hn, wsrc, c0, ncols, dst):
        c = self.c
        done = 0
        while done < ncols:
            w = min(512, ncols - done)
            wb = self.load_w_block(wsrc, c0 + done, w, self.wf_ring, self.wb_ring)
            for tt in range(self.NCH):
                g = (tt * 128) // self.NT
                ps = self.pb[self.pbi % 4]
                self.pbi += 1
                for k in range(8):
                    c.mm(ps[:, 0:w], hn.k(g)[:, k, tt * 128:(tt + 1) * 128], wb[:, k, 0:w], start=(k == 0), stop=(k == 7))
                st = self.stb_ring.next()
                e = "act" if (self.pbi % 2 == 0) else "dve"
                c.copy(e, st[:, 0:w], ps[:, 0:w])
                c.dma(SQ, dst[tt * 128:(tt + 1) * 128, done:done + w], st[:, 0:w])
            done += w

    def load_hn_resident(self):
        c = self.c
        hn = c.sb("hnres", [128, 8, self.TC], BF16)
        for g in range(self.NG):
            col = slice(g * self.NT, (g + 1) * self.NT)
            c.dma("sp", hn.k(g)[:, :, col], self.hnT[:, col].re("(k p) n -> p k n", p=128))
        return hn

    def dense_tail(self, li, last):
        c, NT, NG = self.c, self.NT, self.NG
        inp = self.inp
        with c.scope():
            hn2 = c.sb("hn2", [128, 8, self.TC], BF16)
            with c.scope():
                self.ep_alloc()
                gam = self.load_gamma(inp["ln_ffn"][li, :])
                wo = c.sb("wo", [128, 12, D], BF16)
                with c.scope():
                    wst = c.ring("wost", 2, [128, 3, D], F32)
                    for q in range(4):
                        s = wst.next()
                        c.dma("sp", s[:], inp["w_out"][li, q * 384:(q + 1) * 384, :].re("(k p) n -> p k n", p=128))
                        c.copy("act", wo[:, q * 3:(q + 1) * 3, :], s[:])
                mr = c.ring("mixg", 2, [128, 12, NT], BF16)
                hr = c.ring("hg", 2, [128, 8, NT], F32)
                for g in range(NG):
                    col = slice(g * NT, (g + 1) * NT)
                    mg = mr.next()
                    c.dma("sp", mg[:], self.mixT[:, col].re("(k p) n -> p k n", p=128))
                    hg = hr.next()
                    c.dma("sp", hg[:], self.hT[:, col].re("(k p) n -> p k n", p=128))
                    for dc in range(8):
                        ps = self.pb[dc % 4]
                        for k in range(12):
                            c.mm(ps[:, 0:NT], wo[:, k, dc * 128:(dc + 1) * 128], mg[:, k, :], start=(k == 0), stop=(k == 11))
                        c.tt("dve", hg[:, dc, :], ps[:, 0:NT], hg[:, dc, :], ALU.add)
                    self.epilogue(hg, g, gam, hn2, NT)
            with c.scope():
                self.wf_ring = c.ring("wf", 2, [128, 8, 512], F32)
                self.wb_ring = c.ring("wb", 2, [128, 8, 512], BF16)
                rr = c.ring("relu", 3, [128, NT], BF16)
                ar = c.ring("aout", 3, [128, NT], BF16)
                pbi = 0
                for cbk in range(DFF // 512):
                    wb = self.load_w_block(inp["ffn_w1"][li], cbk * 512, 512, self.wf_ring, self.wb_ring)
                    for sub in range(4):
                        for g in range(NG):
                            ps = self.pb[pbi % 4]
                            pbi += 1
                            for k in range(8):
                                c.mm(ps[:, 0:NT], wb[:, k, sub * 128:(sub + 1) * 128], hn2.k(g)[:, k, g * NT:(g + 1) * NT], start=(k == 0), stop=(k == 7))
                            r = rr.next()
                            c.act(r[:], ps[:, 0:NT], AF.Relu)
                            a = ar.next()
                            c.tt("dve", a[:], ps[:, 0:NT], r[:], ALU.mult)
                            r0 = cbk * 512 + sub * 128
                            c.dma(SQ, self.aT[r0:r0 + 128, g * NT:(g + 1) * NT], a[:])
        with c.scope():
            self.ep_alloc(final=last)
            if last:
                gam = self.load_gamma(inp["ln_final"][:])
            else:
                gam = self.load_gamma(inp["ln_mix"][li + 1, :])
            w2 = c.sb("w2", [128, 32, D], BF16)
            with c.scope():
                wst = c.ring("w2st", 2, [128, 4, D], F32)
                for q in range(8):
                    s = wst.next()
                    c.dma("sp", s[:], inp["ffn_w2"][li, q * 512:(q + 1) * 512, :].re("(k p) n -> p k n", p=128))
                    c.copy("act", w2[:, q * 4:(q + 1) * 4, :], s[:])
            agr = c.ring("ag", 2, [128, 16, NT], BF16)
            hr = c.ring("hg", 2, [128, 8, NT], F32)
            for g in range(NG):
                col = slice(g * NT, (g + 1) * NT)
                ags = []
                for hf in range(2):
                    ag = agr.next()
                    c.dma("sp", ag[:], self.aT[hf * 2048:(hf + 1) * 2048, col].re("(k p) n -> p k n", p=128))
                    ags.append(ag)
                hg = hr.next()
                c.dma("sp", hg[:], self.hT[:, col].re("(k p) n -> p k n", p=128))
                for dc in range(8):
                    ps = self.pb[dc % 4]
                    for k in range(32):
                        c.mm(ps[:, 0:NT], w2[:, k, dc * 128:(dc + 1) * 128], ags[k // 16][:, k % 16, :], start=(k == 0), stop=(k == 31))
                    c.tt("dve", hg[:, dc, :], ps[:, 0:NT], hg[:, dc, :], ALU.add)
                self.epilogue(hg, g, gam, self.hnT, NT, final=last, store_h=not last)

    def zero_mix(self, r0, r1):
        c = self.c
        with c.scope():
            z = c.sb("zmix", [128, self.TC], BF16)
            c.memset("pool", z[:], 0.0)
            for r in range(r0, r1, 128):
                c.dma(SQ, self.mixT[r:r + 128, :], z[:])

    def build(self):
        c = self.c
        with c.es:
            self.declare()
            self.consts()
            self.phase0()
            for li, (typ, j) in enumerate(self.layers):
                last = li == len(self.layers) - 1
                self.mixer_layer(li, typ, j)
                self.dense_tail(li, last)
            c.finish()
        return self.nc

    def mixer_layer(self, li, typ, j):
        if self.cfg.get("stub_mix", False):
            self.zero_mix(0, MIXW)
            return
        raise NotImplementedError


def rotary_tables(pos0, TC):
    half = 64
    inv = (10000.0 ** (-np.arange(half, dtype=np.float32) / half)).astype(np.float32)
    tpos = np.arange(pos0, pos0 + TC, dtype=np.float32)
    ang = (tpos[:, None] * inv[None, :]).astype(np.float32)
    cos = np.cos(ang).astype(np.float32).T
    sin = np.sin(ang).astype(np.float32).T
    cos2 = np.concatenate([cos, cos], axis=0)
    sin2 = np.concatenate([-sin, sin], axis=0)
    return np.ascontiguousarray(cos2), np.ascontiguousarray(sin2)


def make_coef(segs, ci):
    n = len(segs)
    co = np.zeros((1, 64), np.float32)
    sid, k, ns = segs[ci]
    for r in range(n):
        sr, kr, _ = segs[r]
        if sr != sid:
            continue
        if kr == k - 1:
            co[0, r] = 1.0
        if kr == k + 1:
            co[0, 8 + r] = 1.0
        if kr < k:
            co[0, 16 + r] = 1.0
        if kr > k:
            co[0, 40 + r] = 1.0
    for r in range(n):
        sr, kr, _ = segs[r]
        fwd_after = not (sr == sid and kr < k) and any(segs[q][0] == sid and segs[q][1] < k and q < r for q in range(n))
        co[0, 24 + r] = 1.0 if fwd_after else 0.0
        bwd_after = not (sr == sid and kr > k) and any(segs[q][0] == sid and segs[q][1] > k and q > r for q in range(n))
        co[0, 32 + r] = 1.0 if bwd_after else 0.0
    return co


def prep_in_maps(inputs, cfg, xs, segs, pos0s, used):
    cst = make_consts()
    shared = {k: np.ascontiguousarray(np.asarray(v, dtype=np.float32)) for k, v in inputs.items()
              if k not in ("x_prompt", "x_sample") and k in used}
    shared["cst"] = cst
    in_maps = []
    for ci in range(cfg["n_cores"]):
        m = dict(shared)
        m["x"] = np.ascontiguousarray(xs[ci], dtype=np.float32)
        cos2, sin2 = rotary_tables(pos0s[ci], cfg["TC"])
        m["rot_cos"] = cos2
        m["rot_sin"] = sin2
        m["coef"] = make_coef(segs, ci)
        in_maps.append({k: v for k, v in m.items() if k in used})
    return in_maps


FULL_LAYERS = [("AB", 0), ("CD", 0), ("AB", 1), ("CD", 1)]


def kernel(**inputs):
    TC = 4096
    cfg = {"TC": TC, "n_cores": 8, "layers": FULL_LAYERS}
    xp = np.asarray(inputs["x_prompt"], dtype=np.float32)
    xsmp = np.asarray(inputs["x_sample"], dtype=np.float32)
    xs, segs, pos0s = [], [], []
    for b in range(2):
        for k in range(2):
            xs.append(xp[b, k * TC:(k + 1) * TC]); segs.append((b, k, 2)); pos0s.append(k * TC)
    for k in range(4):
        xs.append(xsmp[0, k * TC:(k + 1) * TC]); segs.append((2, k, 4)); pos0s.append(k * TC)
    prog = Prog(cfg)
    nc = prog.build()
    in_maps = prep_in_maps(inputs, cfg, xs, segs, pos0s, set(prog.inp.keys()))
    res = run_bass_kernel_spmd(nc, in_maps, core_ids=list(range(8)))
    ys = [np.asarray(r["y"], dtype=np.float32) for r in res.results]
    y_prompt = np.stack([np.concatenate(ys[0:2], axis=0), np.concatenate(ys[2:4], axis=0)], axis=0)
    y_sample = np.concatenate(ys[4:8], axis=0)[None]
    return (y_prompt, y_sample)


class Prog(Prog):
    def mixer_layer(self, li, typ, j):
        if self.cfg.get("stub_mix", False):
            self.zero_mix(0, MIXW)
            return
        self.pbi = 0
        if typ == "CD":
            self.layer_cd(li, j)
        else:
            self.layer_ab(li, j)

    def proj_scope_alloc(self):
        c = self.c
        self.wf_ring = c.ring("wf", 2, [128, 8, 512], F32)
        self.wb_ring = c.ring("wb", 2, [128, 8, 512], BF16)
        self.st_ring = c.ring("st", 4, [128, 512], F32)
        self.stb_ring = c.ring("stb", 4, [128, 512], BF16)

    def pipeline(self, gens, depth):
        it = iter(gens)
        active = []
        while True:
            while len(active) < depth:
                g = next(it, None)
                if g is None:
                    break
                active.append(g)
            if not active:
                break
            for g in list(active):
                try:
                    next(g)
                except StopIteration:
                    active.remove(g)

    def scratch(self, name, shape, dtype):
        if not hasattr(self, "_scr"):
            self._scr = {}
        if name not in self._scr:
            self._scr[name] = self.c.dram(name, shape, dtype)
        return self._scr[name]

    def bcast_row(self, name, src_row, n):
        t = self.c.sb(name, [128, n], F32)
        self.c.dma("sp", t[:], src_row.re("(o n) -> o n", o=1).pbc(128))
        return t

    def layer_cd(self, li, j):
        c, TC = self.c, self.TC
        W = self.inp["cd_w_in"][j]
        tm_i = self.scratch("tm_i", [TC, 512], BF16)
        tm_g = self.scratch("tm_g", [TC, 512], BF16)
        tm_v = self.scratch("tm_v", [TC, 1024], BF16)
        tm_rg = self.scratch("tm_rg", [TC, 1024], BF16)
        with c.scope():
            hn = self.load_hn_resident()
            self.proj_scope_alloc()
            self.proj_fm(hn, W, 0, 3072, 0, None)
            self.proj_fm(hn, W, 4096, 1024, 4096, None)
            self.proj_tm(hn, W, 3072, 512, tm_i)
            self.proj_tm(hn, W, 3584, 512, tm_g)
            self.proj_tm(hn, W, 5120, 1024, tm_v)
            self.proj_tm(hn, W, 6144, 1024, tm_rg)
        self.gla(li, j, tm_i, tm_g)
        self.retention(li, j, tm_v, tm_rg)

    def gla(self, li, j, tm_i, tm_g):
        c, TC, NT, NG, NCH = self.c, self.TC, self.NT, self.NG, self.NCH
        NC64 = TC // 64
        inp = self.inp
        g_q = self.scratch("g_q", [2, 8, 128, TC], BF16)
        g_k = self.scratch("g_k", [2, 8, 128, TC], BF16)
        g_ktm = self.scratch("g_ktm", [TC, 2048], BF16)
        g_E = self.scratch("g_E", [2, 8, 128, 2, NC64], F32)
        g_S = self.scratch("g_S", [2, NC64, 128, 512], BF16)
        ncd = self.ncd
        with c.scope():
            lbt = c.sb("lbt", [128, 2, ncd, 8], F32)
            for d in range(2):
                for l in range(ncd):
                    c.dma("sp", lbt[:, d, l, :], inp["hg_lb"][d, l, :].re("(c p) -> p c", p=128), nonc=True)
            ex = c.sb("lbex", [128, 2, ncd, 8], F32)
            c.act(ex[:], lbt[:], AF.Exp)
            den = c.sb("lbden", [128, 2, 8], F32)
            c.copy("dve", den[:], ex[:, :, 0, :])
            for l in range(1, ncd):
                c.tt("dve", den[:], den[:], ex[:, :, l, :], ALU.add)
            c.recip(den[:], den[:])
            lb = c.sb("lb", [128, 2, 8], F32)
            c.memset("dve", lb[:], 0.0)
            for l in range(1, j + 1):
                c.tt("dve", lb[:], lb[:], ex[:, :, l, :], ALU.add)
            c.tt("dve", lb[:], lb[:], den[:], ALU.mult)
            omlb = c.sb("omlb", [128, 2, 8], F32)
            c.ts("dve", omlb[:], lb[:], -1.0, ALU.mult, 1.0, ALU.add)
            fr = c.ring("gfl", 2, [128, NT], F32)
            qr = c.ring("gq", 2, [128, NT], F32)
            w1 = c.ring("gw1", 2, [128, NT], F32)
            w2 = c.ring("gw2", 2, [128, NT], F32)
            w3 = c.ring("gw3", 2, [128, NT], F32)
            w4 = c.ring("gw4", 2, [128, NT], F32)
            qo = c.ring("gqo", 2, [128, NT], BF16)
            ko = c.ring("gko", 2, [128, NT], BF16)
            eo = c.ring("geo", 2, [128, 2, NT // 64], F32)
            ktm = c.ring("gktm", 2, [128, NT // 128, 128], BF16)
            ncg = NT // 64
            for d in range(2):
                for h in range(8):
                    for g in range(NG):
                        col = slice(2 + g * NT, 2 + (g + 1) * NT)
                        fl = fr.next()
                        c.dma("sp", fl[:], self.uT[1024 + d * 1024 + h * 128:1024 + d * 1024 + (h + 1) * 128, col])
                        q = qr.next()
                        c.dma("sp", q[:], self.uT[h * 128:(h + 1) * 128, col])
                        f = w1.next()
                        c.act(f[:], fl[:], AF.Sigmoid)
                        c.ts("dve", f[:], f[:], omlb[:, d, h:h + 1], ALU.mult, lb[:, d, h:h + 1], ALU.add)
                        lg = w2.next()
                        c.act(lg[:], f[:], AF.Ln)
                        kk = w3.next()
                        c.ts("dve", kk[:], f[:], -1.0, ALU.mult, 1.0, ALU.add)
                        b = w4.next()
                        c.scan(b[:], self.rm64[:, 0:NT], lg[:], 0.0, ALU.mult, ALU.add)
                        b3 = b[:].re("p (c l) -> p c l", l=64)
                        if d == 1:
                            c.tt("dve", lg[:], lg[:], b[:], ALU.subtract)
                            c.tt("dve", b3, lg[:].re("p (c l) -> p c l", l=64), b3[:, :, 63:64].bc([128, ncg, 64]), ALU.add)
                        e = eo.next()
                        c.act(e[:, 0, :], b3[:, :, 32], AF.Exp)
                        rel = f
                        c.tt("dve", rel[:].re("p (c l) -> p c l", l=64), b3, b3[:, :, 32:33].bc([128, ncg, 64]), ALU.subtract)
                        rel3 = rel[:].re("p (c l) -> p c l", l=64)
                        endi = 63 if d == 0 else 0
                        c.act(e[:, 1, :], rel3[:, :, endi], AF.Exp)
                        c.dma(SQ, g_E[d, h, :, :, g * ncg:(g + 1) * ncg], e[:])
                        E = lg
                        c.act(E[:], rel[:], AF.Exp)
                        Ei = b
                        c.act(Ei[:], rel[:], AF.Exp, scale=-1.0)
                        qt = qo.next()
                        c.tt("dve", qt[:], q[:], E[:], ALU.mult)
                        kt = ko.next()
                        c.tt("dve", kt[:], kk[:], Ei[:], ALU.mult)
                        c.dma(SQ, g_q[d, h, :, g * NT:(g + 1) * NT], qt[:])
                        c.dma(SQ, g_k[d, h, :, g * NT:(g + 1) * NT], kt[:])
                        kT = ktm.next()
                        psb = self.pq[3].k(0)[:, 0:512].bitcast(BF16)
                        for tt in range(NT // 128):
                            c.tr(psb[:, tt * 128:(tt + 1) * 128], kt[:, tt * 128:(tt + 1) * 128], self.cB("ident"))
                        c.copy("act", kT[:], psb[:, 0:NT].re("p (a b) -> p a b", b=128))
                        c.dma(SQ, g_ktm[g * NT:(g + 1) * NT, (d * 8 + h) * 128:(d * 8 + h + 1) * 128].re("(a p) n -> p a n", p=128), kT[:])
        self.gla_state(g_ktm, tm_i, g_E, g_S, store=True)
        with c.scope():
            nw = self.bcast_row("hgnw", inp["hg_norm_w"][j, :], 512)
            Pm = [c.sb(f"gPm{d}", [128, 8, 128], BF16) for d in range(2)]
            for d in range(2):
                c.memset("pool", Pm[d][:], 0.0)
            qr = c.ring("oq", 2, [128, 2, 8, 128], BF16)
            kr = c.ring("ok", 2, [128, 2, 8, 128], BF16)
            vr = c.ring("ov", 2, [128, 512], BF16)
            sr = c.ring("os", 2, [128, 2, 2, 512], BF16)
            gr = c.ring("og", 2, [128, 512], BF16)
            sqr = c.ring("osq", 2, [128, 512], F32)
            yr = c.ring("oy", 2, [128, 512], F32)
            ybr = c.ring("oyb", 2, [128, 512], BF16)
            sgr = c.ring("osg", 2, [128, 512], F32)
            ssr = c.ring("oss", 2, [128, 8], F32)
            mtr = c.ring("omt", 2, [128, 4, 128], BF16)
            def chunk_body(t):
                col = slice(t * 128, (t + 1) * 128)
                q = qr.next(); k = kr.next(); v = vr.next(); S = sr.next(); gg = gr.next()
                c.dma("sp", q[:], g_q[:, :, :, col].re("d h p n -> p d h n"))
                c.dma("sp", k[:], g_k[:, :, :, col].re("d h p n -> p d h n"))
                c.dma("sp", v[:], tm_i[col, :])
                for d in range(2):
                    c.dma("sp", S[:, d], g_S[d, 2 * t:2 * t + 2, :, :].re("s p n -> p s n"))
                c.dma("sp", gg[:], tm_g[col, :])
                for d in range(2):
                    pg = self.pq[d]
                    for h in range(8):
                        for sub in range(2):
                            r = slice(sub * 64, (sub + 1) * 64)
                            c.mm(pg.k(h // 4)[r, h * 128 + sub * 64:h * 128 + (sub + 1) * 64], k[:, d, h, r], q[:, d, h, r])
                    msk = self.cF("m64_f" if d == 0 else "m64_b")
                    for sub in range(2):
                        r = slice(sub * 64, (sub + 1) * 64)
                        c.tt("dve", Pm[d][r, :, r], pg[r, :].re("p (h t) -> p h t", h=8)[:, :, r],
                             msk[r, r].re("p (o t) -> p o t", o=1).bc([64, 8, 64]), ALU.mult)
                yield
                po = self.pb[4]
                for h in range(8):
                    oc = slice(h * 64, (h + 1) * 64)
                    c.mm(po[:, oc], Pm[0][:, h, :], v[:, oc], start=True, stop=False)
                    c.mm(po[:, oc], Pm[1][:, h, :], v[:, oc], start=False, stop=False)
                    for d in range(2):
                        for sub in range(2):
                            r = slice(sub * 64, (sub + 1) * 64)
                            c.mm(po[r, oc], q[:, d, h, r], S[:, d, sub, oc], start=False, stop=(d == 1 and sub == 1))
                sq = sqr.next()
                c.act(sq[:], po[:, :], AF.Square)
                ss = ssr.next()
                c.reduce(ss[:], sq[:].re("p (h v) -> p h v", h=8))
                c.ts("dve", ss[:], ss[:], 1.0 / 64, ALU.mult, EPS, ALU.add)
                c.act(ss[:], ss[:], AF.Sqrt)
                c.recip(ss[:], ss[:])
                y = yr.next()
                c.tt("dve", y[:].re("p (h v) -> p h v", h=8), po[:, :].re("p (h v) -> p h v", h=8),
                     ss[:].re("p (h o) -> p h o", o=1).bc([128, 8, 64]), ALU.mult)
                c.tt("dve", y[:], y[:], nw[:], ALU.mult)
                sg = sgr.next()
                c.act(sg[:], gg[:], AF.Sigmoid)
                yb = ybr.next()
                c.tt("dve", yb[:], y[:], sg[:], ALU.mult)
                psb = self.pq[3].k(0)[:, 0:512].bitcast(BF16)
                for kc in range(4):
                    c.tr(psb[:, kc * 128:(kc + 1) * 128], yb[:, kc * 128:(kc + 1) * 128], self.cB("ident"))
                mt = mtr.next()
                c.copy("act", mt[:], psb[:, 0:512].re("p (a b) -> p a b", b=128))
                c.dma(SQ, self.mixT[0:512, col].re("(a p) n -> p a n", p=128), mt[:])
                yield
            self.pipeline((chunk_body(t) for t in range(NCH)), 1)

    def gla_state(self, g_ktm, tm_i, g_E, g_S, store):
        c, TC = self.c, self.TC
        NC64 = TC // 64
        with c.scope():
            Et = c.sb("gEt", [128, 2, 8, 2, NC64], F32)
            c.dma("sp", Et[:], g_E[:].re("d h p e n -> p d h e n"))
            H = [c.sb(f"gH{d}", [128, 8, 64], F32) for d in range(2)]
            tmp = c.ring("gtmp", 2, [128, 8, 64], F32)
            hb = c.ring("gHb", 4, [128, 512], BF16)
            kr = c.ring("skt", 3, [128, 1024], BF16)
            vr = c.ring("svt", 3, [128, 512], BF16)
            for d in range(2):
                c.memset("pool", H[d][:], 0.0)
                order = range(NC64) if d == 0 else range(NC64 - 1, -1, -1)
                kt = vt = None
                cur_tile = -1
                for c64 in order:
                    t = c64 // 2
                    sub = c64 % 2
                    if t != cur_tile:
                        cur_tile = t
                        kt = kr.next(); vt = vr.next()
                        c.dma("sp", kt[:], g_ktm[t * 128:(t + 1) * 128, d * 1024:(d + 1) * 1024])
                        c.dma("sp", vt[:], tm_i[t * 128:(t + 1) * 128, :])
                    tp = tmp.next()
                    c.tt("dve", tp[:], H[d][:], Et[:, d, :, 0, c64:c64 + 1].bc([128, 8, 64]), ALU.mult)
                    if store:
                        b = hb.next()
                        c.copy("act", b[:], tp[:].re("p h v -> p (h v)"))
                        c.dma(SQ, g_S[d, c64, :, :], b[:])
                    ps = self.pb[d]
                    r = slice(sub * 64, (sub + 1) * 64)
                    for h in range(8):
                        c.mm(ps[:, h * 64:(h + 1) * 64], kt[r, h * 128:(h + 1) * 128], vt[r, h * 64:(h + 1) * 64])
                    c.tt("dve", tp[:], tp[:], ps[:, :].re("p (h v) -> p h v", h=8), ALU.add)
                    c.tt("dve", H[d][:], tp[:], Et[:, d, :, 1, c64:c64 + 1].bc([128, 8, 64]), ALU.mult)

    def retention(self, li, j, tm_v, tm_rg):
        self.zero_mix(512, MIXW)


class Prog(Prog):
    def retention(self, li, j, tm_v, tm_rg):
        c, TC, NT, NG, NCH = self.c, self.TC, self.NT, self.NG, self.NCH
        inp = self.inp
        r_q = self.scratch("r_q", [4, 128, TC], BF16)
        r_k = self.scratch("r_k", [4, 128, TC], BF16)
        r_qd = self.scratch("r_qd", [2, 4, 128, TC], BF16)
        r_kd = self.scratch("r_kd", [TC, 1024], BF16)
        r_S = self.scratch("r_S", [2, NCH, 128, 1024], BF16)
        with c.scope():
            ld = c.sb("rld", [128, 8], F32)
            c.dma("sp", ld[:], inp["ret_log_decay"][j].re("(o d) h -> o (d h)", o=1).pbc(128))
            LG = c.sb("rLG", [128, 8], F32)
            c.act(LG[:], ld[:], AF.Exp)
            c.ts("dve", LG[:], LG[:], -1.0, ALU.mult)
            DT = c.sb("rDT", [128, 2, 4, 128], BF16)
            GQ = c.sb("rGQ", [128, 2, 4, 128], F32)
            KD = c.sb("rKD", [128, 8], F32)
            GL = c.sb("rGL", [128, 8], F32)
            for d in range(2):
                sfx = "_f" if d == 0 else "_b"
                for h in range(4):
                    i = d * 4 + h
                    c.act(DT[:, d, h, :], self.cF("rel" + sfx), AF.Exp, scale=LG[:, i:i + 1])
                    c.act(GQ[:, d, h, :], self.cF("posq" + sfx), AF.Exp, scale=LG[:, i:i + 1])
                    c.act(KD[:, i:i + 1], self.cf[:, NCST * 128 + d:NCST * 128 + d + 1], AF.Exp, scale=LG[:, i:i + 1])
            c.act(GL[:], LG[:], AF.Exp, scale=128.0)
            self.rGL = GL
            with c.scope():
                cosr = c.sb("rcos", [128, TC], F32)
                sinr = c.sb("rsin", [128, TC], F32)
                c.dma("sp", cosr[:], inp["rot_cos"][:, :])
                c.dma("sp", sinr[:], inp["rot_sin"][:, :])
                xr = c.ring("rx", 2, [128, NT], F32)
                t1r = c.ring("rt1", 2, [128, NT], F32)
                t2r = c.ring("rt2", 2, [128, NT], F32)
                ror = c.ring("rro", 2, [128, NT], BF16)
                qdr = c.ring("rqd", 2, [128, NT], BF16)
                kdr = c.ring("rkdt", 2, [128, NT // 128, 2, 128], BF16)
                for which in range(2):
                    for h in range(4):
                        for g in range(NG):
                            col = slice(g * NT, (g + 1) * NT)
                            x = xr.next()
                            r0 = 4096 + which * 512 + h * 128
                            c.dma("sp", x[:], self.uT[r0:r0 + 128, 2 + g * NT:2 + (g + 1) * NT])
                            ps = self.pb[(which * 4 + h) % 4]
                            c.mm(ps[:, 0:NT], self.cF("swap"), x[:])
                            sc = 1.0 if which == 0 else 128.0 ** -0.5
                            t1 = t1r.next()
                            c.stt(t1[:], x[:], sc, cosr[:, col], ALU.mult, ALU.mult)
                            t2 = t2r.next()
                            c.stt(t2[:], ps[:, 0:NT], sc, sinr[:, col], ALU.mult, ALU.mult)
                            ro = ror.next()
                            c.tt("dve", ro[:], t1[:], t2[:], ALU.add)
                            if which == 0:
                                c.dma(SQ, r_q[h, :, col], ro[:])
                                for d in range(2):
                                    qd = qdr.next()
                                    c.tt("dve" if d == 0 else "pool", qd[:].re("p (a b) -> p a b", b=128), ro[:].re("p (a b) -> p a b", b=128),
                                         GQ[:, d, h, :].re("p (o b) -> p o b", o=1).bc([128, NT // 128, 128]), ALU.mult)
                                    c.dma(SQ, r_qd[d, h, :, col], qd[:])
                            else:
                                c.dma(SQ, r_k[h, :, col], ro[:])
                                psb = self.pq[3].k(0)[:, 0:512].bitcast(BF16)
                                for tt in range(NT // 128):
                                    c.tr(psb[:, tt * 128:(tt + 1) * 128], ro[:, tt * 128:(tt + 1) * 128], self.cB("ident"))
                                kd = kdr.next()
                                for d in range(2):
                                    i = d * 4 + h
                                    c.act(kd[:, :, d, :], psb[:, 0:NT].re("p (a b) -> p a b", b=128), AF.Copy, scale=KD[:, i:i + 1])
                                for d in range(2):
                                    c.dma(SQ, r_kd[col, d * 512 + h * 128:d * 512 + (h + 1) * 128].re("(a p) n -> p a n", p=128), kd[:, :, d, :])
            with c.scope():
                H = [c.sb(f"rS{d}", [128, 1024], F32) for d in range(2)]
                for d in range(2):
                    c.memset("pool", H[d][:], 0.0)
                if self.NC > 1:
                    self.ret_state(H, r_kd, tm_v, r_S, store=False)
                    with c.scope():
                        Ms = c.sb("rMs", [128, 8], F32)
                        c.act(Ms[:], LG[:], AF.Exp, scale=128.0 * NCH)
                        ctr = c.sb("rctr", [128, 2048], F32)
                        for d in range(2):
                            c.copy("dve", ctr[:, d * 1024:(d + 1) * 1024], H[d][:])
                        xout = self.xchg(ctr[:], 2048, "ret")
                        self.combine_diag(H, xout, 2048, lambda d: d * 1024, 1024,
                                          lambda xr, d: Ms[:, d * 4:(d + 1) * 4].re("p (h o) -> p h o", o=1).bc([128, 4, 256]),
                                          lambda v: v.re("p (h v) -> p h v", h=4))
                self.ret_state(H, r_kd, tm_v, r_S, store=True)
            with c.scope():
                gw = self.bcast_row("rgw", inp["ret_gn_w"][j, :], 1024)
                gb = self.bcast_row("rgb", inp["ret_gn_b"][j, :], 1024)
                qr = c.ring("oq", 2, [128, 4, 128], BF16)
                kr = c.ring("ok", 2, [128, 4, 128], BF16)
                qdr = c.ring("oqd", 2, [128, 2, 4, 128], BF16)
                vr = c.ring("ov", 2, [128, 1024], BF16)
                sr = c.ring("os", 2, [128, 2, 1024], BF16)
                gr = c.ring("og", 2, [128, 1024], BF16)
                pmr = c.ring("opm", 2, [128, 2, 4, 128], BF16)
                ocr = c.ring("ooc", 2, [128, 1024], F32)
                sqr = c.ring("osq", 2, [128, 1024], F32)
                sgr = c.ring("osg", 2, [128, 1024], F32)
                ybr = c.ring("oyb", 2, [128, 1024], BF16)
                str_ = c.ring("ost", 2, [128, 8], F32)
                mtr = c.ring("omt", 2, [128, 8, 128], BF16)
                def chunk_body(t):
                    col = slice(t * 128, (t + 1) * 128)
                    q = qr.next(); k = kr.next(); qd = qdr.next(); v = vr.next(); S = sr.next(); gg = gr.next()
                    c.dma("sp", q[:], r_q[:, :, col].re("h p n -> p h n"))
                    c.dma("sp", k[:], r_k[:, :, col].re("h p n -> p h n"))
                    c.dma("sp", qd[:], r_qd[:, :, :, col].re("d h p n -> p d h n"))
                    c.dma("sp", v[:], tm_v[col, :])
                    c.dma("sp", S[:], r_S[:, t, :, :].re("d p n -> p d n"))
                    c.dma("sp", gg[:], tm_rg[col, :])
                    yield
                    pg = self.pb[0]
                    for h in range(4):
                        c.mm(pg[:, h * 128:(h + 1) * 128], k[:, h, :], q[:, h, :])
                    pm = pmr.next()
                    for d in range(2):
                        c.tt("dve", pm[:, d], pg[:, :].re("p (h t) -> p h t", h=4), DT[:, d], ALU.mult)
                    yield
                    po = self.pq[1]
                    for h in range(4):
                        oc = slice(h * 256, (h + 1) * 256)
                        pk = po.k(h // 2)
                        c.mm(pk[:, oc], pm[:, 0, h, :], v[:, oc], start=True, stop=False)
                        c.mm(pk[:, oc], pm[:, 1, h, :], v[:, oc], start=False, stop=False)
                        c.mm(pk[:, oc], qd[:, 0, h, :], S[:, 0, oc], start=False, stop=False)
                        c.mm(pk[:, oc], qd[:, 1, h, :], S[:, 1, oc], start=False, stop=True)
                    st = str_.next()
                    c.reduce(st[:, 0:4], po[:, :].re("p (h v) -> p h v", h=4))
                    c.ts("dve", st[:, 0:4], st[:, 0:4], 1.0 / 256, ALU.mult)
                    oc_ = ocr.next()
                    c.tt("dve", oc_[:].re("p (h v) -> p h v", h=4), po[:, :].re("p (h v) -> p h v", h=4),
                         st[:, 0:4].re("p (h o) -> p h o", o=1).bc([128, 4, 256]), ALU.subtract)
                    sq = sqr.next()
                    c.act(sq[:], oc_[:], AF.Square)
                    c.reduce(st[:, 4:8], sq[:].re("p (h v) -> p h v", h=4))
                    c.ts("dve", st[:, 4:8], st[:, 4:8], 1.0 / 256, ALU.mult, EPS, ALU.add)
                    c.act(st[:, 4:8], st[:, 4:8], AF.Sqrt)
                    c.recip(st[:, 4:8], st[:, 4:8])
                    c.tt("dve", oc_[:].re("p (h v) -> p h v", h=4), oc_[:].re("p (h v) -> p h v", h=4),
                         st[:, 4:8].re("p (h o) -> p h o", o=1).bc([128, 4, 256]), ALU.mult)
                    c.tt("dve", oc_[:], oc_[:], gw[:], ALU.mult)
                    c.tt("dve", oc_[:], oc_[:], gb[:], ALU.add)
                    sg = sgr.next()
                    c.act(sg[:], gg[:], AF.Silu)
                    yb = ybr.next()
                    c.tt("dve", yb[:], oc_[:], sg[:], ALU.mult)
                    psb = self.pq[3].k(0)[:, 0:512].bitcast(BF16)
                    for kc in range(8):
                        c.tr(psb[:, kc * 128:(kc + 1) * 128], yb[:, kc * 128:(kc + 1) * 128], self.cB("ident"))
                    mt = mtr.next()
                    c.copy("act", mt[:], psb[:, :].re("p (a b) -> p a b", b=128))
                    c.dma(SQ, self.mixT[512:1536, col].re("(a p) n -> p a n", p=128), mt[:])
                    yield
                self.pipeline((chunk_body(t) for t in range(NCH)), 1)

    def ret_state(self, H2, r_kd, tm_v, r_S, store):
        c, NCH = self.c, self.NCH
        GL = self.rGL
        with c.scope():
            S = [H2[d][:].re("p (h v) -> p h v", h=4) for d in range(2)]
            tmp = c.ring("rtmp", 2, [128, 4, 256], F32)
            sb = c.ring("rSb", 4, [128, 1024], BF16)
            kr = c.ring("rskt", 3, [128, 512], BF16)
            vr = c.ring("rsvt", 3, [128, 1024], BF16)
            def dir_body(d):
              order = range(NCH) if d == 0 else range(NCH - 1, -1, -1)
              for t in order:
                if True:
                    rows = slice(t * 128, (t + 1) * 128)
                    kt = kr.next(); vt = vr.next()
                    c.dma("sp", kt[:], r_kd[rows, d * 512:(d + 1) * 512])
                    c.dma("sp", vt[:], tm_v[rows, :])
                    if store:
                        b = sb.next()
                        c.copy("act", b[:], H2[d][:])
                        c.dma(SQ, r_S[d, t, :, :], b[:])
                    po = self.pq[2 + d]
                    for h in range(4):
                        c.mm(po.k(h // 2)[:, h * 256:(h + 1) * 256], kt[:, h * 128:(h + 1) * 128], vt[:, h * 256:(h + 1) * 256])
                    tp = tmp.next()
                    yield
                    c.tt("dve", tp[:], S[d], GL[:, d * 4:(d + 1) * 4].re("p (h o) -> p h o", o=1).bc([128, 4, 256]), ALU.mult)
                    c.tt("dve", S[d], tp[:], po[:, :].re("p (h v) -> p h v", h=4), ALU.add)
                    yield
            self.pipeline([dir_body(0), dir_body(1)], 2)


class Prog(Prog):
    def layer_ab(self, li, j):
        c, TC = self.c, self.TC
        W = self.inp["ab_w_in"][j]
        tm_z = self.scratch("tm_z", [TC, 1024], BF16)
        with c.scope():
            hn = self.load_hn_resident()
            self.proj_scope_alloc()
            if self.cfg.get("poison", False):
                pz = c.sb("poison", [128, TC + 3], F32)
                c.memset("pool", pz[:], 3.0e4)
                for r in range(0, AB_IN, 128):
                    m = min(128, AB_IN - r)
                    c.dma(SQ, self.uT[r:r + m, :], pz[0:m, :])
            if self.NC > 1:
                self.halo_hn = self.halo_prepare(li)
            else:
                self.halo_hn = None
                self.halo_fill(li, j)
            self.proj_fm(hn, W, 0, 1920, 0, None)
            self.proj_fm(hn, W, 2944, 1568, 2944, None)
            self.halo_hn = None
            self.proj_tm(hn, W, 1920, 1024, tm_z)
        if self.cfg.get("skip_rwkv", False):
            self.zero_mix(0, 512)
        else:
            self.rwkv(li, j)
        if self.cfg.get("skip_ssd", False):
            self.zero_mix(512, MIXW)
        else:
            self.ssd(li, j, tm_z)

    def halo_fill(self, li, j):
        c, TC = self.c, self.TC
        z = c.sb("hz", [128, 4], F32)
        c.memset("pool", z[:], 0.0)
        for r in range(0, AB_IN, 128):
            m = min(128, AB_IN - r)
            c.dma(SQ, self.uT[r:r + m, 0:2], z[0:m, 0:2], nonc=True)
            c.dma(SQ, self.uT[r:r + m, TC + 2:TC + 3], z[0:m, 0:1], nonc=True)

    def ssd(self, li, j, tm_z):
        c, TC, NT, NG, NCH = self.c, self.TC, self.NT, self.NG, self.NCH
        inp = self.inp
        XB = 2944
        s_xtm = self.scratch("s_xtm", [TC, 1024], BF16)
        s_Btm = self.scratch("s_Btm", [TC, 256], BF16)
        s_BT = self.scratch("s_BT", [2, 128, TC], BF16)
        s_CT = self.scratch("s_CT", [2, 128, TC], BF16)
        s_aT = self.scratch("s_aT", [64, TC], F32)
        s_dttm = self.scratch("s_dttm", [TC, 64], F32)
        s_atm = self.scratch("s_atm", [TC, 64], F32)
        s_atot = self.scratch("s_atot", [64, NCH], F32)
        s_S = self.scratch("s_S", [2, NCH, 128, 1024], BF16)
        with c.scope():
            cw = c.sb("scw", [128, 12, 4], F32)
            for k in range(4):
                c.dma("sp", cw[:, :, k], inp["ssm_conv_w"][j, k, :].re("(c p) -> p c", p=128), nonc=True)
            cbias = c.sb("scb", [128, 12], F32)
            c.dma("sp", cbias[:], inp["ssm_conv_b"][j, :].re("(c p) -> p c", p=128), nonc=True)
            ur = c.ring("su", 2, [128, NT + 3], F32)
            ar = c.ring("sacc", 2, [128, NT], F32)
            xr = c.ring("sxc", 2, [128, NT], BF16)
            tr_ = c.ring("sxt", 2, [128, NT // 128, 128], BF16)
            for fc in range(12):
                for g in range(NG):
                    u = ur.next()
                    c.dma("sp", u[:], self.uT[XB + fc * 128:XB + (fc + 1) * 128, g * NT:g * NT + NT + 3])
                    acc = ar.next()
                    c.ts("dve", acc[:], u[:, 2:NT + 2], cw[:, fc, 2:3], ALU.mult, cbias[:, fc:fc + 1], ALU.add)
                    c.stt(acc[:], u[:, 0:NT], cw[:, fc, 0:1], acc[:], ALU.mult, ALU.add)
                    c.stt(acc[:], u[:, 1:NT + 1], cw[:, fc, 1:2], acc[:], ALU.mult, ALU.add)
                    c.stt(acc[:], u[:, 3:NT + 3], cw[:, fc, 3:4], acc[:], ALU.mult, ALU.add)
                    xc = xr.next()
                    c.act(xc[:], acc[:], AF.Silu)
                    col = slice(g * NT, (g + 1) * NT)
                    if fc >= 8:
                        dst = s_BT if fc < 10 else s_CT
                        c.dma(SQ, dst[(fc - 8) % 2, :, col], xc[:])
                    if fc < 10:
                        psb = self.pq[3].k(0)[:, 0:512].bitcast(BF16)
                        for tt in range(NT // 128):
                            c.tr(psb[:, tt * 128:(tt + 1) * 128], xc[:, tt * 128:(tt + 1) * 128], self.cB("ident"))
                        xt = tr_.next()
                        c.copy("act", xt[:], psb[:, 0:NT].re("p (a b) -> p a b", b=128))
                        if fc < 8:
                            c.dma(SQ, s_xtm[col, fc * 128:(fc + 1) * 128].re("(a p) n -> p a n", p=128), xt[:])
                        else:
                            c.dma(SQ, s_Btm[col, (fc - 8) * 128:(fc - 7) * 128].re("(a p) n -> p a n", p=128), xt[:])
            dtb = c.sb("sdtb", [64, 1], F32)
            alog = c.sb("salog", [64, 1], F32)
            c.memset("pool", dtb[:], 0.0)
            c.memset("pool", alog[:], 0.0)
            for d in range(2):
                c.dma("sp", dtb[d * 32:d * 32 + 16, :], inp["ssm_dt_bias"][j, d, :].re("(p o) -> p o", o=1), nonc=True)
                c.dma("sp", alog[d * 32:d * 32 + 16, :], inp["ssm_a_log"][j, d, :].re("(p o) -> p o", o=1), nonc=True)
            Aex = c.sb("sAex", [64, 1], F32)
            c.act(Aex[:], alog[:], AF.Exp)
            c.ts("dve", Aex[:], Aex[:], -1.0, ALU.mult)
            dr = c.ring("sdt", 2, [64, NT], F32)
            d2 = c.ring("sda", 2, [64, NT], F32)
            d3 = c.ring("sac", 2, [64, NT], F32)
            tmr = c.ring("stm", 2, [128, NT // 128, 2, 64], F32)
            tot = c.sb("stot", [64, NCH], F32)
            ncg = NT // 128
            for g in range(NG):
                col = slice(g * NT, (g + 1) * NT)
                dt = dr.next()
                c.memset("pool", dt[:], 0.0)
                for d in range(2):
                    r0 = 4480 + d * 16
                    c.dma("sp", dt[d * 32:d * 32 + 16, :], self.uT[r0:r0 + 16, 2 + g * NT:2 + (g + 1) * NT])
                c.act(dt[:], dt[:], AF.Exp, bias=dtb[:, 0:1])
                c.act(dt[:], dt[:], AF.Ln, bias=1.0)
                da = d2.next()
                c.ts("dve", da[:], dt[:], Aex[:, 0:1], ALU.mult)
                acs = d3.next()
                c.scan(acs[:], self.rm128[0:64, 0:NT], da[:], 0.0, ALU.mult, ALU.add)
                a3 = acs[:].re("p (c l) -> p c l", l=128)
                c.tt("dve", da[32:64, :], da[32:64, :], acs[32:64, :], ALU.subtract)
                c.tt("dve", a3[32:64], da[32:64, :].re("p (c l) -> p c l", l=128), a3[32:64, :, 127:128].bc([32, ncg, 128]), ALU.add)
                c.copy("dve", tot[0:32, g * ncg:(g + 1) * ncg], a3[0:32, :, 127])
                c.copy("dve", tot[32:64, g * ncg:(g + 1) * ncg], a3[32:64, :, 0])
                c.dma(SQ, s_aT[:, col], acs[:])
                tm = tmr.next()
                ps = self.pb[4]
                for tt in range(ncg):
                    c.tr(ps[:, tt * 128:tt * 128 + 64], dt[:, tt * 128:(tt + 1) * 128], self.cF("ident")[0:64, 0:64])
                    c.tr(ps[:, tt * 128 + 64:tt * 128 + 128], acs[:, tt * 128:(tt + 1) * 128], self.cF("ident")[0:64, 0:64])
                c.copy("act", tm[:], ps[:, 0:NT].re("p (a e n) -> p a e n", e=2, n=64))
                c.dma(SQ, s_dttm[col, :].re("(a p) n -> p a n", p=128), tm[:, :, 0, :])
                c.dma(SQ, s_atm[col, :].re("(a p) n -> p a n", p=128), tm[:, :, 1, :])
            c.dma(SQ, s_atot[:, :], tot[:])
        with c.scope():
            H = [c.sb(f"sH{d}", [128, 1024], F32) for d in range(2)]
            for d in range(2):
                c.memset("pool", H[d][:], 0.0)
            if self.NC > 1:
                self.ssd_state(H, s_xtm, s_Btm, s_dttm, s_atm, s_atot, s_S, store=False)
                with c.scope():
                    atb = c.sb("xatb", [128, 64, NCH], F32)
                    c.dma("sp", atb[:], s_atot[:, :].re("(o r) n -> o r n", o=1).pbc(128))
                    c.act(atb[:], atb[:], AF.Exp)
                    ctr = c.sb("sctr", [128, 3072], F32)
                    c.memset("pool", ctr[:], 0.0)
                    for d in range(2):
                        c.copy("dve", ctr[:, d * 1536:d * 1536 + 1024], H[d][:])
                        c.reduce(ctr[:, d * 1536 + 1024:d * 1536 + 1040], atb[:, d * 32:d * 32 + 16, :], op=ALU.mult)
                    xout = self.xchg(ctr[:], 3072, "ssd")
                    self.combine_diag(H, xout, 3072, lambda d: d * 1536, 1024,
                                      lambda xr, d: xr[:, d * 1536 + 1024:d * 1536 + 1040].re("p (h o) -> p h o", o=1).bc([128, 16, 64]),
                                      lambda v: v.re("p (h v) -> p h v", h=16))
            self.ssd_state(H, s_xtm, s_Btm, s_dttm, s_atm, s_atot, s_S, store=True)
        with c.scope():
            nw = self.bcast_row("snw", inp["ssm_norm_w"][j, :], 1024)
            dsk = self.bcast_row("sdsk", inp["ssm_d"][j, :], 16)
            xr = c.ring("ox", 2, [128, 1024], BF16)
            dtr = c.ring("odt", 2, [128, 64], F32)
            atr = c.ring("oat", 2, [128, 64], F32)
            btr = c.ring("obt", 2, [128, 2, 128], BF16)
            ctr = c.ring("oct", 2, [128, 2, 128], BF16)
            abr = c.ring("oabc", 2, [128, 2, 16, 128], F32)
            hr = c.ring("oh", 2, [128, 2, 1024], BF16)
            zr = c.ring("oz", 2, [128, 1024], BF16)
            cbr = c.ring("ocb", 2, [128, 2, 128], BF16)
            X = c.ring("oX", 2, [128, 16, 128], F32)
            Dm = c.ring("oDm", 2, [128, 16, 128], BF16)
            Pm = c.ring("oPm", 2, [128, 16, 128], BF16)
            xdt = c.ring("oxdt", 2, [128, 1024], BF16)
            ea = c.ring("oea", 2, [128, 16], F32)
            t0r = c.ring("ot0", 2, [128, 1024], F32)
            t1r = c.ring("ot1", 2, [128, 1024], F32)
            yr = c.ring("oyy", 2, [128, 1024], F32)
            ybr = c.ring("oyb", 2, [128, 1024], BF16)
            str_ = c.ring("ost", 2, [128, 2], F32)
            mtr = c.ring("omt", 2, [128, 8, 128], BF16)
            def chunk_body(t):
                col = slice(t * 128, (t + 1) * 128)
                x = xr.next(); dtt = dtr.next(); at = atr.next(); bt = btr.next(); ct = ctr.next()
                abc = abr.next(); H = hr.next(); z = zr.next()
                c.dma("sp", x[:], s_xtm[col, :])
                c.dma("sp", dtt[:], s_dttm[col, :])
                c.dma("sp", at[:], s_atm[col, :])
                c.dma("sp", bt[:], s_BT[:, :, col].re("g p n -> p g n"))
                c.dma("sp", ct[:], s_CT[:, :, col].re("g p n -> p g n"))
                for d in range(2):
                    c.dma("sp", abc[:, d], s_aT[d * 32:d * 32 + 16, col].re("(o h) n -> o h n", o=1).pbc(128))
                c.dma("sp", H[:], s_S[:, t, :, :].re("d p n -> p d n"))
                c.dma("sp", z[:], tm_z[col, :])
                yield
                pcb = self.pb[6]
                for g in range(2):
                    c.mm(pcb[:, g * 128:(g + 1) * 128], bt[:, g, :], ct[:, g, :])
                cbs = cbr.next()
                c.copy("act", cbs[:], pcb[:, 0:256].re("p (g t) -> p g t", g=2))
                pms, xds = [], []
                for d in range(2):
                    Xt = X.next()
                    neg = self.cF("neg_f" if d == 0 else "neg_b")
                    c.tt("dve", Xt[:], abc[:, d], neg.re("p (o t) -> p o t", o=1).bc([128, 16, 128]), ALU.add)
                    c.tt("dve", Xt[:], Xt[:], at[:, d * 32:d * 32 + 16].re("p (h o) -> p h o", o=1).bc([128, 16, 128]), ALU.subtract)
                    D_ = Dm.next()
                    c.act(D_[:], Xt[:], AF.Exp)
                    P_ = Pm.next()
                    c.tt("dve", P_[:].re("p (g e) t -> p g e t", g=2), D_[:].re("p (g e) t -> p g e t", g=2),
                         cbs[:].re("p g (o t) -> p g o t", o=1).bc([128, 2, 8, 128]), ALU.mult)
                    xd = xdt.next()
                    c.tt("dve", xd[:].re("p (h v) -> p h v", h=16), x[:].re("p (h v) -> p h v", h=16),
                         dtt[:, d * 32:d * 32 + 16].re("p (h o) -> p h o", o=1).bc([128, 16, 64]), ALU.mult)
                    pms.append(P_); xds.append(xd)
                yield
                po = self.pq[0]
                for h in range(16):
                    oc = slice(h * 64, (h + 1) * 64)
                    pk = po.k(h // 8)
                    c.mm(pk[:, oc], pms[0][:, h, :], xds[0][:, oc], start=True, stop=False)
                    c.mm(pk[:, oc], pms[1][:, h, :], xds[1][:, oc], start=False, stop=True)
                ts_ = []
                for d in range(2):
                    pf = self.pq[1 + d]
                    for g in range(2):
                        c.mm(pf.k(g)[:, g * 512:(g + 1) * 512], ct[:, g, :], H[:, d, g * 512:(g + 1) * 512])
                    e = ea.next()
                    c.act(e[:], at[:, d * 32:d * 32 + 16], AF.Exp)
                    tx = (t0r if d == 0 else t1r).next()
                    c.tt("dve", tx[:].re("p (h v) -> p h v", h=16), pf[:, :].re("p (h v) -> p h v", h=16),
                         e[:].re("p (h o) -> p h o", o=1).bc([128, 16, 64]), ALU.mult)
                    ts_.append(tx)
                c.tt("dve", ts_[0][:], ts_[0][:], ts_[1][:], ALU.add)
                c.tt("dve", ts_[1][:].re("p (h v) -> p h v", h=16), x[:].re("p (h v) -> p h v", h=16),
                     dsk[:].re("p (h o) -> p h o", o=1).bc([128, 16, 64]), ALU.mult)
                c.tt("dve", ts_[0][:], ts_[0][:], ts_[1][:], ALU.add)
                y = yr.next()
                c.tt("dve", y[:], po[:, :], ts_[0][:], ALU.add)
                sz = ts_[1]
                c.act(sz[:], z[:], AF.Silu)
                c.tt("dve", y[:], y[:], sz[:], ALU.mult)
                sq = ts_[0]
                c.act(sq[:], y[:], AF.Square)
                st = str_.next()
                c.reduce(st[:], sq[:].re("p (g v) -> p g v", g=2))
                c.ts("dve", st[:], st[:], 1.0 / 512, ALU.mult, EPS, ALU.add)
                c.act(st[:], st[:], AF.Sqrt)
                c.recip(st[:], st[:])
                c.tt("dve", y[:].re("p (g v) -> p g v", g=2), y[:].re("p (g v) -> p g v", g=2),
                     st[:].re("p (g o) -> p g o", o=1).bc([128, 2, 512]), ALU.mult)
                yb = ybr.next()
                c.tt("dve", yb[:], y[:], nw[:], ALU.mult)
                psb = self.pq[3].k(1)[:, 512:1024].bitcast(BF16)
                for kc in range(8):
                    c.tr(psb[:, kc * 128:(kc + 1) * 128], yb[:, kc * 128:(kc + 1) * 128], self.cB("ident"))
                mt = mtr.next()
                c.copy("act", mt[:], psb[:, :].re("p (a b) -> p a b", b=128))
                c.dma(SQ, self.mixT[512:1536, col].re("(a p) n -> p a n", p=128), mt[:])
                yield
            self.pipeline((chunk_body(t) for t in range(NCH)), 1)

    def ssd_state(self, H2, s_xtm, s_Btm, s_dttm, s_atm, s_atot, s_S, store):
        c, NCH = self.c, self.NCH
        with c.scope():
            atb = c.sb("satb", [128, 64, NCH], F32)
            c.dma("sp", atb[:], s_atot[:, :].re("(o r) n -> o r n", o=1).pbc(128))
            eat = c.sb("seat", [128, 64, NCH], F32)
            c.act(eat[:], atb[:], AF.Exp)
            H = [H2[d][:].re("p (h v) -> p h v", h=16) for d in range(2)]
            hb = c.ring("sHb", 4, [128, 1024], BF16)
            xr = c.ring("ssx", 3, [128, 1024], BF16)
            br = c.ring("ssb", 3, [128, 256], BF16)
            dr = c.ring("ssd", 3, [128, 64], F32)
            ar = c.ring("ssa", 3, [128, 64], F32)
            fr = c.ring("ssf", 2, [128, 16], F32)
            xdr = c.ring("ssxd", 2, [128, 1024], BF16)
            tmp = c.ring("sstmp", 2, [128, 16, 64], F32)
            def dir_body(d):
              order = range(NCH) if d == 0 else range(NCH - 1, -1, -1)
              for t in order:
                if True:
                    rows = slice(t * 128, (t + 1) * 128)
                    x = xr.next(); b = br.next(); dt = dr.next(); a = ar.next()
                    c.dma("sp", x[:], s_xtm[rows, :])
                    c.dma("sp", b[:], s_Btm[rows, :])
                    c.dma("sp", dt[:], s_dttm[rows, :])
                    c.dma("sp", a[:], s_atm[rows, :])
                    if store:
                        hbb = hb.next()
                        c.copy("act", hbb[:], H2[d][:])
                        c.dma(SQ, s_S[d, t, :, :], hbb[:])
                    f = fr.next()
                    c.tt("dve", f[:], atb[:, d * 32:d * 32 + 16, t], a[:, d * 32:d * 32 + 16], ALU.subtract)
                    c.act(f[:], f[:], AF.Exp)
                    c.tt("dve", f[:], f[:], dt[:, d * 32:d * 32 + 16], ALU.mult)
                    xd = xdr.next()
                    c.tt("dve", xd[:].re("p (h v) -> p h v", h=16), x[:].re("p (h v) -> p h v", h=16),
                         f[:].re("p (h o) -> p h o", o=1).bc([128, 16, 64]), ALU.mult)
                    po = self.pq[2 + d]
                    for g in range(2):
                        c.mm(po.k(g)[:, g * 512:(g + 1) * 512], b[:, g * 128:(g + 1) * 128], xd[:, g * 512:(g + 1) * 512])
                    tp = tmp.next()
                    yield
                    c.tt("dve", tp[:], H[d], eat[:, d * 32:d * 32 + 16, t].re("p (h o) -> p h o", o=1).bc([128, 16, 64]), ALU.mult)
                    c.tt("dve", H[d], tp[:], po[:, :].re("p (h v) -> p h v", h=16), ALU.add)
                    yield
            self.pipeline([dir_body(0), dir_body(1)], 2)

    def rwkv(self, li, j):
        self.zero_mix(0, 512)


class Prog(Prog):
    def rwkv(self, li, j):
        c, TC, NT, NG, NCH = self.c, self.TC, self.NT, self.NG, self.NCH
        inp = self.inp
        ncg = NT // 128
        w_fm = self.scratch("w_fm", [2, 4, 4, 128, TC], BF16)
        w_vtm = self.scratch("w_vtm", [TC, 512], BF16)
        w_g = self.scratch("w_g", [4, 128, TC], BF16)
        w_bonus = self.scratch("w_bonus", [4, 128, TC], F32)
        w_E = self.scratch("w_E", [2, 128, 4, 2, NCH], F32)
        w_P = self.scratch("w_P", [2, NCH, 2, 128, 8, 128], BF16)
        w_U = self.scratch("w_U", [2, NCH, 128, 8, 64], BF16)
        w_Q = self.scratch("w_Q", [2, NCH, 128, 4, 128], BF16)
        w_M = self.scratch("w_M", [2, NCH, 128, 4, 128], BF16)
        w_N = self.scratch("w_N", [2, NCH, 128, 4, 64], F32)
        w_S = self.scratch("w_S", [2, NCH, 128, 512], BF16)
        with c.scope():
            def vec(name, src, n):
                t = c.sb(name, [128, n], F32)
                c.dma("sp", t[:], src.re("(c p) -> p c", p=128), nonc=True)
                return t
            mu0 = vec("mu0", inp["rw_mu"][j, 0, :], 15)
            mu1 = vec("mu1", inp["rw_mu"][j, 1, :], 15)
            c0 = c.sb("muc0", [128, 15], F32)
            c.tt("dve", c0[:], mu0[:], mu1[:], ALU.add)
            c.ts("dve", c0[:], c0[:], -1.0, ALU.mult, 1.0, ALU.add)
            w0 = [vec(f"w0{d}", inp["rw_w0"][j, d, :], 4) for d in range(2)]
            a0 = [vec(f"a0{d}", inp["rw_a0"][j, d, :], 4) for d in range(2)]
            k_k = vec("k_k", inp["rw_k_k"][j, :], 4)
            k_a = vec("k_a", inp["rw_k_a"][j, :], 4)
            r_k = vec("r_k", inp["rw_r_k"][j, :], 4)
            omka = c.sb("omka", [128, 4], F32)
            c.ts("dve", omka[:], k_a[:], -1.0, ALU.mult, 1.0, ALU.add)
            lw_f = c.sb("lorast", [128, 512], F32)
            w2b = c.sb("w2b", [128, 512], BF16)
            a2b = c.sb("a2b", [128, 512], BF16)
            g2b = c.sb("g2b", [128, 512], BF16)
            for dst, src in ((w2b, inp["rw_w2"][j].re("d r c -> (d r) c")), (a2b, inp["rw_a2"][j].re("d r c -> (d r) c")), (g2b, inp["rw_g2"][j])):
                c.dma("sp", lw_f[:], src)
                c.copy("dve", dst[:], lw_f[:])
            ur = c.ring("wu", 2, [128, NT + 2], F32)
            us = [c.sb(f"us{i}", [128, NT], F32) for i in range(15)]
            twb = c.sb("twb", [128, NT], BF16)
            alb = c.sb("alb", [128, NT], BF16)
            sgb = c.sb("sgb", [128, NT], BF16)
            kkt = [c.sb(f"kk{q}", [128, NT], F32) for q in range(4)]
            wk = [c.ring(f"wk{i}", 2, [128, NT], F32) for i in range(6)]
            ob = c.ring("wob", 6, [128, NT], BF16)
            of = c.ring("wof", 2, [128, NT], F32)
            eo = c.ring("weo", 2, [128, 2, ncg], F32)
            vtr = c.ring("wvt", 2, [128, ncg, 128], BF16)
            for g in range(NG):
                col = slice(g * NT, (g + 1) * NT)
                for fc in range(15):
                    u = ur.next()
                    c.dma("sp", u[:], self.uT[fc * 128:(fc + 1) * 128, g * NT + 1:g * NT + NT + 3])
                    c.ts("dve", us[fc][:], u[:, 1:NT + 1], c0[:, fc:fc + 1], ALU.mult)
                    c.stt(us[fc][:], u[:, 0:NT], mu0[:, fc:fc + 1], us[fc][:], ALU.mult, ALU.add)
                    c.stt(us[fc][:], u[:, 2:NT + 2], mu1[:, fc:fc + 1], us[fc][:], ALU.mult, ALU.add)
                c.act(twb[:], us[12][:], AF.Tanh)
                c.copy("dve", alb[:], us[13][:])
                c.act(sgb[:], us[14][:], AF.Sigmoid)
                for q in range(4):
                    cs = slice(q * 128, (q + 1) * 128)
                    ps = self.pb[0]
                    c.mm(ps[:, 0:NT], g2b[:, cs], sgb[:])
                    o = ob.next()
                    c.copy("act", o[:], ps[:, 0:NT])
                    c.dma(SQ, w_g[q, :, col], o[:])
                    t1 = wk[0].next()
                    c.stt(t1[:], us[q][:], r_k[:, q:q + 1], us[4 + q][:], ALU.mult, ALU.mult)
                    ps = self.pb[1]
                    c.mm(ps[:, 0:NT], self.cF("blk64"), t1[:])
                    bo = of.next()
                    c.tt("dve", bo[:], ps[:, 0:NT], us[8 + q][:], ALU.mult)
                    c.dma(SQ, w_bonus[q, :, col], bo[:])
                    kr = wk[1].next()
                    c.ts("dve", kr[:], us[4 + q][:], k_k[:, q:q + 1], ALU.mult)
                    sq = wk[2].next()
                    c.act(sq[:], kr[:], AF.Square)
                    ps = self.pb[2]
                    c.mm(ps[:, 0:NT], self.cF("blk64"), sq[:])
                    nr = wk[3].next()
                    c.act(nr[:], ps[:, 0:NT], AF.Sqrt)
                    c.ts("dve", nr[:], nr[:], 1e-12, ALU.max)
                    c.recip(nr[:], nr[:])
                    c.tt("dve", kkt[q][:], kr[:], nr[:], ALU.mult)
                    vb = ob.next()
                    c.copy("dve", vb[:], us[8 + q][:])
                    psb = self.pq[3].k(0)[:, 0:512].bitcast(BF16)
                    for tt in range(ncg):
                        c.tr(psb[:, tt * 128:(tt + 1) * 128], vb[:, tt * 128:(tt + 1) * 128], self.cB("ident"))
                    vt = vtr.next()
                    c.copy("act", vt[:], psb[:, 0:NT].re("p (a b) -> p a b", b=128))
                    c.dma(SQ, w_vtm[col, q * 128:(q + 1) * 128].re("(a p) n -> p a n", p=128), vt[:])
                for d in range(2):
                    P = slice(d * 64, (d + 1) * 64)
                    for q in range(4):
                        cs = slice(q * 128, (q + 1) * 128)
                        ps = self.pb[4]
                        c.mm(ps[:, 0:NT], w2b[P, cs], twb[P, :])
                        lw = wk[0].next()
                        c.act(lw[:], ps[:, 0:NT], AF.Sigmoid, bias=w0[d][:, q:q + 1])
                        c.ts("dve", lw[:], lw[:], -math.exp(-0.5), ALU.mult)
                        ps2 = self.pb[5]
                        c.mm(ps2[:, 0:NT], a2b[P, cs], alb[P, :])
                        av = wk[1].next()
                        c.act(av[:], ps2[:, 0:NT], AF.Sigmoid, bias=a0[d][:, q:q + 1])
                        b = wk[2].next()
                        c.scan(b[:], self.rm128[:, 0:NT], lw[:], 0.0, ALU.mult, ALU.add)
                        b3 = b[:].re("p (c l) -> p c l", l=128)
                        if d == 1:
                            t = wk[3].next()
                            c.tt("dve", t[:], lw[:], b[:], ALU.subtract)
                            c.tt("dve", b3, t[:].re("p (c l) -> p c l", l=128), b3[:, :, 127:128].bc([128, ncg, 128]), ALU.add)
                        e = eo.next()
                        c.act(e[:, 0, :], b3[:, :, 64], AF.Exp)
                        rel = wk[3].next()
                        c.tt("dve", rel[:].re("p (c l) -> p c l", l=128), b3, b3[:, :, 64:65].bc([128, ncg, 128]), ALU.subtract)
                        endi = 127 if d == 0 else 0
                        c.act(e[:, 1, :], rel[:].re("p (c l) -> p c l", l=128)[:, :, endi], AF.Exp)
                        c.dma(SQ, w_E[d, :, q, :, g * ncg:(g + 1) * ncg], e[:])
                        E = wk[4].next()
                        c.act(E[:], rel[:], AF.Exp)
                        Ei = wk[5].next()
                        c.act(Ei[:], rel[:], AF.Exp, scale=-1.0)
                        c.tt("dve", rel[:], rel[:], lw[:], ALU.subtract)
                        Ex = lw
                        c.act(Ex[:], rel[:], AF.Exp)
                        o = ob.next()
                        c.tt("dve", o[:], us[q][:], E[:], ALU.mult)
                        c.dma(SQ, w_fm[d, 0, q, :, col], o[:])
                        o = ob.next()
                        c.tt("dve", o[:], kkt[q][:], Ex[:], ALU.mult)
                        c.dma(SQ, w_fm[d, 1, q, :, col], o[:])
                        ka = b
                        c.tt("dve", ka[:], kkt[q][:], av[:], ALU.mult)
                        o = ob.next()
                        c.stt(o[:], ka[:], -1.0, Ei[:], ALU.mult, ALU.mult)
                        c.dma(SQ, w_fm[d, 2, q, :, col], o[:])
                        c.ts("dve", av[:], av[:], k_a[:, q:q + 1], ALU.mult, omka[:, q:q + 1], ALU.add)
                        c.tt("dve", av[:], av[:], us[4 + q][:], ALU.mult)
                        o = ob.next()
                        c.tt("dve", o[:], av[:], Ei[:], ALU.mult)
                        c.dma(SQ, w_fm[d, 3, q, :, col], o[:])
        if self.cfg.get("rwkv_stop", 9) < 2:
            self.zero_mix(0, 512)
            return
        with c.scope():
            I2 = c.sb("wI2", [128, 64], F32)
            c.copy("dve", I2[0:64, :], self.cF("ident")[0:64, 0:64])
            c.copy("dve", I2[64:128, :], self.cF("ident")[64:128, 64:128])
            fmr = c.ring("w2fm", 2, [128, 4, 2, 128], BF16)
            zr = c.ring("w2z", 2, [128, 2, 2, 2, 128], BF16)
            for zb in zr.bufs:
                c.memset("pool", zb[:], 0.0)
            vr = c.ring("w2v", 2, [128, 512], BF16)
            X0r = c.ring("w2X0", 2, [128, 4, 128], F32)
            Xfr = c.ring("w2Xf", 4, [128, 4, 128], F32)
            Yfr = c.ring("w2Yf", 3, [128, 4, 128], F32)
            Tfr = c.ring("w2Tf", 3, [128, 4, 128], F32)
            Tr = c.ring("w2T", 2, [128, 4, 128], BF16)
            Pr = c.ring("w2P", 2, [128, 2, 4, 128], BF16)
            Br = c.ring("w2B", 2, [128, 4, 128], BF16)
            TMr = c.ring("w2TM", 2, [128, 3, 2, 128], BF16)
            Yin = c.ring("w2Yin", 2, [128, 4, 128], BF16)
            WUr = c.ring("w2WU", 2, [128, 4, 128], BF16)
            Utr = c.ring("w2Ut", 2, [128, 4, 64], BF16)
            Qr = c.ring("w2Q", 2, [128, 2, 128], BF16)
            Mr = c.ring("w2M", 2, [128, 2, 128], BF16)
            for mb in Mr.bufs:
                c.memset("pool", mb[:], 0.0)
            Nr = c.ring("w2N", 2, [128, 2, 64], F32)
            pb = self.pb
            for d in range(2):
                mS_Y = self.cF("maskS_f" if d == 0 else "maskS_b")
                mS_X = self.cF("maskS_b" if d == 0 else "maskS_f")
                mI = self.cF("maskI_f" if d == 0 else "maskI_b")
                def chunk_body(t):
                    col = slice(t * 128, (t + 1) * 128)
                    v = vr.next()
                    c.dma("sp", v[:], w_vtm[col, :])
                    for Q in range(2):
                        fm = fmr.next()
                        for kind in range(4):
                            c.dma("sp", fm[:, kind], w_fm[d, kind, 2 * Q:2 * Q + 2, :, col].re("q p n -> p q n"))
                        RT, BT_, AL, KH = 0, 1, 2, 3
                        Z = zr.next()
                        for zi, kind in enumerate((2, 3)):
                            for hp in range(2):
                                Pp = slice(hp * 64, (hp + 1) * 64)
                                c.dma("sp", Z[Pp, zi, :, hp, :], w_fm[d, kind, 2 * Q:2 * Q + 2, Pp, col].re("q p n -> p q n"))
                        for hl in range(4):
                            ql, hp = hl // 2, hl % 2
                            hs = slice(hl * 128, (hl + 1) * 128)
                            c.mm(pb[0][:, hs], fm[:, BT_, ql, :], Z[:, 0, ql, hp, :])
                            c.mm(pb[1][:, hs], Z[:, 0, ql, hp, :], fm[:, BT_, ql, :])
                            c.mm(pb[2][:, hs], Z[:, 0, ql, hp, :], fm[:, RT, ql, :])
                            c.mm(pb[3][:, hs], Z[:, 1, ql, hp, :], fm[:, BT_, ql, :])
                            c.mm(pb[4][:, hs], Z[:, 1, ql, hp, :], fm[:, RT, ql, :])

                        def b4(m):
                            return m.re("p (o t) -> p o t", o=1).bc([128, 4, 128])

                        def v4(ps):
                            return ps[:, :].re("p (h t) -> p h t", h=4)
                        Pt = Pr.next(); Bt = Br.next()
                        X0 = X0r.next(); Xb = Xfr.next(); Yb = Yfr.next(); Tt = Tfr.next()
                        c.tt("dve", X0[:], v4(pb[0]), b4(mS_X), ALU.mult)
                        c.tt("dve", Yb[:], v4(pb[1]), b4(mS_Y), ALU.mult)
                        c.tt("dve", Xb[:], X0[:], b4(self.cF("bd16")), ALU.mult)
                        c.tt("dve", Yb[:], Yb[:], b4(self.cF("bd16")), ALU.mult)
                        c.tt("dve", Tt[:], Yb[:], b4(self.cF("ident")), ALU.add)
                        c.tt("dve", Pt[:, 0], v4(pb[2]), b4(mI), ALU.mult)
                        c.tt("dve", Bt[:], v4(pb[3]), b4(mS_Y), ALU.mult)
                        c.tt("dve", Pt[:, 1], v4(pb[4]), b4(mI), ALU.mult)
                        Xc, Yc = Xb, Yb
                        for lev in range(3):
                            Xn = Xfr.next()
                            for hl in range(4):
                                c.mm(pb[0][:, hl * 128:(hl + 1) * 128], Yc[:, hl, :], Xc[:, hl, :])
                            c.copy("act", Xn[:], v4(pb[0]))
                            for hl in range(4):
                                c.mm(pb[1][:, hl * 128:(hl + 1) * 128], Xn[:, hl, :], Tt[:, hl, :])
                            Tn = Tfr.next()
                            c.tt("dve", Tn[:], v4(pb[1]), Tt[:], ALU.add)
                            Tt = Tn
                            if lev < 2:
                                Yn = Yfr.next()
                                for hl in range(4):
                                    c.tr(pb[2][:, hl * 128:(hl + 1) * 128], Xn[:, hl, :], self.cF("ident"))
                                c.copy("act", Yn[:], v4(pb[2]))
                                Yc = Yn
                            Xc = Xn
                        for nm in ("off16", "off32", "off64"):
                            Xo = Xfr.next()
                            c.tt("dve", Xo[:], X0[:], b4(self.cF(nm)), ALU.mult)
                            for hl in range(4):
                                c.mm(pb[0][:, hl * 128:(hl + 1) * 128], Xo[:, hl, :], Tt[:, hl, :])
                            G = Yfr.next()
                            c.copy("act", G[:], v4(pb[0]))
                            for hl in range(4):
                                c.tr(pb[2][:, hl * 128:(hl + 1) * 128], Tt[:, hl, :], self.cF("ident"))
                            Tb = Xfr.next()
                            c.copy("act", Tb[:], v4(pb[2]))
                            for hl in range(4):
                                c.mm(pb[1][:, hl * 128:(hl + 1) * 128], Tb[:, hl, :], G[:, hl, :])
                            Tn = Tfr.next()
                            c.tt("dve", Tn[:], v4(pb[1]), Tt[:], ALU.add)
                            Tt = Tn
                        Ttb = Tr.next()
                        c.copy("dve", Ttb[:], Tt[:])
                        Tt = Ttb
                        psb = self.pq[2].k(1)[:, 512:1024].bitcast(BF16)
                        for i, kind in enumerate((BT_, AL, KH)):
                            for ql in range(2):
                                c.tr(psb[:, (i * 2 + ql) * 128:(i * 2 + ql + 1) * 128], fm[:, kind, ql, :], self.cB("ident"))
                        TM = TMr.next()
                        c.copy("act", TM[:], psb[:, 0:768].re("p (k q n) -> p k q n", k=3, q=2))
                        for hl in range(4):
                            h = Q * 4 + hl
                            c.mm(pb[6][:, hl * 64:(hl + 1) * 64], Bt[:, hl, :], v[:, h * 64:(h + 1) * 64])
                        yin = Yin.next()
                        c.copy("act", yin[:, :, 64:128], pb[6][:, 0:256].re("p (h v) -> p h v", h=4))
                        c.copy("dve", yin[:, :, 0:64], TM[:, 0].re("p q (e j) -> p (q e) j", e=2))
                        for hl in range(4):
                            c.mm(pb[3][:, hl * 128:(hl + 1) * 128], Tt[:, hl, :], yin[:, hl, :])
                        WU = WUr.next()
                        c.copy("act", WU[:], v4(pb[3]))
                        Ut = Utr.next()
                        c.copy("dve", Ut[:], WU[:, :, 64:128])
                        for hl in range(4):
                            ql, hp = hl // 2, hl % 2
                            Pp = slice(hp * 64, (hp + 1) * 64)
                            h = Q * 4 + hl
                            c.mm(pb[4][Pp, ql * 128:(ql + 1) * 128], WU[:, hl, 0:64], Pt[:, 0, hl, :])
                            c.mm(pb[7][Pp, ql * 64:(ql + 1) * 64], WU[:, hl, 0:64], TM[:, 1, ql, Pp])
                            c.mm(pb[7][Pp, 128 + ql * 64:128 + (ql + 1) * 64], TM[:, 1, ql, Pp], WU[:, hl, 64:128], start=True, stop=False)
                            c.mm(pb[7][Pp, 128 + ql * 64:128 + (ql + 1) * 64], TM[:, 2, ql, Pp], v[:, h * 64:(h + 1) * 64], start=False, stop=True)
                        Qc = Qr.next()
                        c.tt("dve", Qc[:], pb[4][:, 0:256].re("p (q t) -> p q t", q=2), fm[:, RT], ALU.add)
                        Mt = Mr.next()
                        for hp in range(2):
                            Pp = slice(hp * 64, (hp + 1) * 64)
                            c.tt("dve", Mt[Pp, :, Pp], pb[7][Pp, 0:128].re("p (q j) -> p q j", q=2),
                                 I2[Pp, :].re("p (o j) -> p o j", o=1).bc([64, 2, 64]), ALU.add)
                        Np = Nr.next()
                        c.copy("act", Np[:], pb[7][:, 128:256].re("p (q j) -> p q j", q=2))
                        c.dma(SQ, w_P[d, t, :, :, 4 * Q:4 * Q + 4, :].re("k s h n -> s k h n"), Pt[:])
                        c.dma(SQ, w_U[d, t, :, 4 * Q:4 * Q + 4, :], Ut[:])
                        c.dma(SQ, w_Q[d, t, :, 2 * Q:2 * Q + 2, :], Qc[:])
                        c.dma(SQ, w_M[d, t, :, 2 * Q:2 * Q + 2, :], Mt[:])
                        c.dma(SQ, w_N[d, t, :, 2 * Q:2 * Q + 2, :], Np[:])
                    yield
                self.pipeline((chunk_body(t) for t in range(NCH)), 1)
        if self.cfg.get("rwkv_stop", 9) < 3:
            self.zero_mix(0, 512)
            return
        with c.scope():
            I2 = c.sb("wI2s", [128, 64], F32)
            c.copy("dve", I2[0:64, :], self.cF("ident")[0:64, 0:64])
            c.copy("dve", I2[64:128, :], self.cF("ident")[64:128, 64:128])
            H = [c.sb(f"wH{d}", [128, 4, 128], F32) for d in range(2)]
            self.rwkv_init_state(H, I2)
            if self.NC > 1:
                self.rwkv_state(H, w_E, w_M, w_N, w_S, store=False)
                self.rwkv_exchange(H, I2)
            self.rwkv_state(H, w_E, w_M, w_N, w_S, store=True)
        with c.scope():
            gw = self.bcast_row("wgw", inp["rw_gn_w"][j, :], 512)
            gb = self.bcast_row("wgb", inp["rw_gn_b"][j, :], 512)
            Pr = c.ring("o4P", 2, [128, 2, 2, 8, 128], BF16)
            Ur = c.ring("o4U", 2, [128, 2, 8, 64], BF16)
            Qr = c.ring("o4Q", 2, [128, 2, 4, 128], BF16)
            Sr = c.ring("o4S", 2, [128, 2, 4, 2, 64], BF16)
            vr = c.ring("o4v", 2, [128, 512], BF16)
            gr = c.ring("o4g", 2, [128, 4, 128], BF16)
            bor = c.ring("o4b", 2, [128, 4, 128], F32)
            ocr = c.ring("o4oc", 2, [128, 512], F32)
            sqr = c.ring("o4sq", 2, [128, 512], F32)
            ybr = c.ring("o4yb", 2, [128, 512], BF16)
            str_ = c.ring("o4st", 2, [128, 16], F32)
            t1r = c.ring("o4t1", 2, [128, 4, 128], F32)
            mtr = c.ring("o4mt", 2, [128, 4, 128], BF16)
            def chunk_body(t):
                col = slice(t * 128, (t + 1) * 128)
                Pt = Pr.next(); Ut = Ur.next(); Qc = Qr.next(); S = Sr.next(); v = vr.next(); gg = gr.next(); bo = bor.next()
                for d in range(2):
                    for kind in range(2):
                        c.dma("sp", Pt[:, d, kind], w_P[d, t, kind, :, :, :])
                    c.dma("sp", Ut[:, d], w_U[d, t, :, :, :])
                    c.dma("sp", Qc[:, d], w_Q[d, t, :, :, :])
                    c.dma("sp", S[:, d], w_S[d, t, :, :].re("p (q e v) -> p q e v", q=4, e=2))
                c.dma("sp", v[:], w_vtm[col, :])
                c.dma("sp", gg[:], w_g[:, :, col].re("q p n -> p q n"))
                c.dma("sp", bo[:], w_bonus[:, :, col].re("q p n -> p q n"))
                yield
                po = self.pb[0]
                first = True
                for h in range(8):
                    q, hp = h // 2, h % 2
                    Pp = slice(hp * 64, (hp + 1) * 64)
                    oc = slice(h * 64, (h + 1) * 64)
                    for d in range(2):
                        c.mm(po[:, oc], Pt[:, d, 0, h, :], Ut[:, d, h, :], start=first, stop=False)
                        first = False
                        c.mm(po[:, oc], Pt[:, d, 1, h, :], v[:, oc], start=False, stop=False)
                        c.mm(po[:, oc], Qc[:, d, q, :], S[:, d, q, hp, :], start=False, stop=(h == 7 and d == 1))
                st = str_.next()
                c.reduce(st[:, 0:8], po[:, :].re("p (h v) -> p h v", h=8))
                c.ts("dve", st[:, 0:8], st[:, 0:8], 1.0 / 64, ALU.mult)
                oc_ = ocr.next()
                c.tt("dve", oc_[:].re("p (h v) -> p h v", h=8), po[:, :].re("p (h v) -> p h v", h=8),
                     st[:, 0:8].re("p (h o) -> p h o", o=1).bc([128, 8, 64]), ALU.subtract)
                sq = sqr.next()
                c.act(sq[:], oc_[:], AF.Square)
                c.reduce(st[:, 8:16], sq[:].re("p (h v) -> p h v", h=8))
                c.ts("dve", st[:, 8:16], st[:, 8:16], 1.0 / 64, ALU.mult, 64e-5, ALU.add)
                c.act(st[:, 8:16], st[:, 8:16], AF.Sqrt)
                c.recip(st[:, 8:16], st[:, 8:16])
                c.tt("dve", oc_[:].re("p (h v) -> p h v", h=8), oc_[:].re("p (h v) -> p h v", h=8),
                     st[:, 8:16].re("p (h o) -> p h o", o=1).bc([128, 8, 64]), ALU.mult)
                c.tt("dve", oc_[:], oc_[:], gw[:], ALU.mult)
                yb = ybr.next()
                c.tt("dve", yb[:], oc_[:], gb[:], ALU.add)
                psb = self.pq[3].k(0)[:, 0:512].bitcast(BF16)
                for kc in range(4):
                    c.tr(psb[:, kc * 128:(kc + 1) * 128], yb[:, kc * 128:(kc + 1) * 128], self.cB("ident"))
                t1 = t1r.next()
                c.tt("dve", t1[:], psb[:, 0:512].re("p (a b) -> p a b", b=128), bo[:], ALU.add)
                mt = mtr.next()
                c.tt("dve", mt[:], t1[:], gg[:], ALU.mult)
                c.dma(SQ, self.mixT[0:512, col].re("(a p) n -> p a n", p=128), mt[:])
                yield
            self.pipeline((chunk_body(t) for t in range(NCH)), 1)

    def rwkv_state(self, H, w_E, w_M, w_N, w_S, store):
        c, NCH = self.c, self.NCH
        with c.scope():
            Et = c.sb("wEt", [128, 2, 4, 2, NCH], F32)
            for d in range(2):
                c.dma("sp", Et[:, d], w_E[d])
            tmp = c.ring("wtmp", 2, [128, 4, 128], F32)
            hb = c.ring("wHb", 4, [128, 4, 128], BF16)
            hz = c.ring("wHz", 4, [128, 4, 2, 64], BF16)
            for zb in hz.bufs:
                c.memset("pool", zb[:], 0.0)
            mr = c.ring("wsm", 3, [128, 4, 128], BF16)
            nr = c.ring("wsn", 3, [128, 4, 64], F32)
            def dir_body(d):
              order = range(NCH) if d == 0 else range(NCH - 1, -1, -1)
              for t in order:
                if True:
                    M = mr.next(); N = nr.next()
                    c.dma("sp", M[:], w_M[d, t])
                    c.dma("sp", N[:], w_N[d, t])
                    tp = tmp.next()
                    c.tt("dve", tp[:], H[d][:], Et[:, d, :, 0, t:t + 1].bc([128, 4, 128]), ALU.mult)
                    b = hb.next()
                    c.copy("act", b[:], tp[:])
                    if store:
                        bz = hz.next()
                        for hp in range(2):
                            Pp = slice(hp * 64, (hp + 1) * 64)
                            c.copy("dve", bz[Pp, :, hp, :], b[Pp, :, 0:64])
                        c.dma(SQ, w_S[d, t, :, :], bz[:].re("p q e v -> p (q e v)"))
                    ps = self.pb[6 + d]
                    for q in range(4):
                        c.mm(ps[:, q * 128:(q + 1) * 128], M[:, q, :], b[:, q, :])
                    ps3 = ps[:, :].re("p (q v) -> p q v", q=4)
                    c.tt("dve", tp[:, :, 0:64], ps3[:, :, 0:64], N[:], ALU.add)
                    c.copy("act", tp[:, :, 64:128], ps3[:, :, 64:128])
                    c.tt("dve", H[d][:], tp[:], Et[:, d, :, 1, t:t + 1].bc([128, 4, 128]), ALU.mult)
                    yield
            self.pipeline([dir_body(0), dir_body(1)], 2)

    def rwkv_init_state(self, H, I2):
        c = self.c
        for d in range(2):
            c.memset("pool", H[d][:, :, 0:64], 0.0)
            c.copy("dve", H[d][:, :, 64:128], I2[:].re("p (o j) -> p o j", o=1).bc([128, 4, 64]))

    def rwkv_exchange(self, H, I2):
        c = self.c
        co = self.coef_tile()
        with c.scope():
            ctr = c.sb("wctr", [128, 1024], F32)
            for d in range(2):
                c.copy("dve", ctr[:, d * 512:(d + 1) * 512], H[d][:].re("p q v -> p (q v)"))
            xout = self.xchg(ctr[:], 1024, "rwkv")
            self.rwkv_init_state(H, I2)
            xr_ring = c.ring("wxr", 2, [128, 1024], F32)
            bdr = c.ring("wbd", 2, [128, 4, 128], F32)
            for zb in bdr.bufs:
                c.memset("pool", zb[:], 0.0)
            ltr = c.ring("wlt", 2, [128, 4, 128], BF16)
            hbr = c.ring("whb", 2, [128, 4, 64], BF16)
            t1r = c.ring("wt1", 2, [128, 4, 64], F32)
            for step in range(self.NC):
                for d in range(2):
                    r = step if d == 0 else self.NC - 1 - step
                    xr = xr_ring.next()
                    self.load_rank(xr, xout, r)
                    X3 = xr[:, d * 512:(d + 1) * 512].re("p (q v) -> p q v", q=4)
                    a = co[:, 16 + r:17 + r] if d == 0 else co[:, 40 + r:41 + r]
                    b = co[:, 24 + r:25 + r] if d == 0 else co[:, 32 + r:33 + r]
                    BD = bdr.next()
                    for hp in range(2):
                        Pp = slice(hp * 64, (hp + 1) * 64)
                        c.copy("dve", BD[Pp, :, Pp], X3[Pp, :, 64:128])
                    pst = self.pb[4]
                    for q in range(4):
                        c.tr(pst[:, q * 128:(q + 1) * 128], BD[:, q, :], self.cF("ident"))
                    LT = ltr.next()
                    c.copy("act", LT[:], pst[:, :].re("p (q v) -> p q v", q=4))
                    Hb = hbr.next()
                    c.copy("dve", Hb[:], H[d][:, :, 0:64])
                    ps = self.pb[5]
                    for q in range(4):
                        c.mm(ps[:, q * 64:(q + 1) * 64], LT[:, q, :], Hb[:, q, :])
                    t1 = t1r.next()
                    c.tt("dve", t1[:], ps[:, 0:256].re("p (q v) -> p q v", q=4), X3[:, :, 0:64], ALU.add)
                    c.ts("dve", t1[:], t1[:], a, ALU.mult)
                    c.stt(H[d][:, :, 0:64], H[d][:, :, 0:64], b, t1[:], ALU.mult, ALU.add)

class Prog(Prog):
    def gla(self, li, j, tm_i, tm_g):
        c, TC, NT, NG, NCH = self.c, self.TC, self.NT, self.NG, self.NCH
        inp = self.inp
        ncg = NT // 128
        g_qin = self.scratch("g_qin", [2, 8, 128, TC], BF16)
        g_ktm = self.scratch("g_ktm", [TC, 2048], BF16)
        g_Pm = self.scratch("g_Pm", [2, 8, NCH, 128, 128], BF16)
        g_E = self.scratch("g_E", [2, 8, 128, NCH], F32)
        g_S = self.scratch("g_S", [2, NCH, 128, 512], BF16)
        ncd = self.ncd
        with c.scope():
            lbt = c.sb("lbt", [128, 2, ncd, 8], F32)
            for d in range(2):
                for l in range(ncd):
                    c.dma("sp", lbt[:, d, l, :], inp["hg_lb"][d, l, :].re("(c p) -> p c", p=128), nonc=True)
            ex = c.sb("lbex", [128, 2, ncd, 8], F32)
            c.act(ex[:], lbt[:], AF.Exp)
            den = c.sb("lbden", [128, 2, 8], F32)
            c.copy("dve", den[:], ex[:, :, 0, :])
            for l in range(1, ncd):
                c.tt("dve", den[:], den[:], ex[:, :, l, :], ALU.add)
            c.recip(den[:], den[:])
            lb = c.sb("lb", [128, 2, 8], F32)
            c.memset("dve", lb[:], 0.0)
            for l in range(1, j + 1):
                c.tt("dve", lb[:], lb[:], ex[:, :, l, :], ALU.add)
            c.tt("dve", lb[:], lb[:], den[:], ALU.mult)
            omlb = c.sb("omlb", [128, 2, 8], F32)
            c.ts("dve", omlb[:], lb[:], -1.0, ALU.mult, 1.0, ALU.add)
            fr = c.ring("gfl", 2, [128, NT], F32)
            qr = c.ring("gq", 2, [128, NT], F32)
            wk = [c.ring(f"gw{i}", 4, [128, NT], F32) for i in range(6)]
            qs = [c.ring(f"gqs{i}", 2, [128, NT], BF16) for i in range(4)]
            ks = [c.ring(f"gks{i}", 2, [128, NT], BF16) for i in range(4)]
            qo = c.ring("gqo", 2, [128, NT], BF16)
            ko = c.ring("gko", 2, [128, NT], BF16)
            eo = c.ring("geo", 2, [128, ncg], F32)
            ktm = c.ring("gktm", 2, [128, ncg, 128], BF16)
            tmpP = c.ring("gtP", 2, [128, 4, 128], F32)
            pmo = c.ring("gpmo", 2, [128, ncg, 128], BF16)
            gens = []
            for d in range(2):
                i0 = CST_NAMES.index("gl16_f" if d == 0 else "gl16_b")
                M4 = self.cf[:, i0 * 128:(i0 + 4) * 128].re("p (l t) -> p l t", l=4)
                for h in range(8):
                    for g in range(NG):
                        gens.append(self.gla_pre_body(d, h, g, M4, (fr, qr, wk, qs, ks, qo, ko, eo, ktm, tmpP, pmo, omlb, lb, g_qin, g_ktm, g_Pm, g_E)))
            self.pipeline(gens, 2)
        with c.scope():
            H = [c.sb(f"gH{d}", [128, 512], F32) for d in range(2)]
            for d in range(2):
                c.memset("pool", H[d][:], 0.0)
            if self.NC > 1:
                self.gla_state(H, g_ktm, tm_i, g_E, g_S, store=False)
                with c.scope():
                    Et = c.sb("gEx", [128, 2, 8, NCH], F32)
                    c.dma("sp", Et[:], g_E[:].re("d h p n -> p d h n"))
                    ctr = c.sb("gctr", [128, 1040], F32)
                    for d in range(2):
                        c.copy("dve", ctr[:, d * 520:d * 520 + 512], H[d][:])
                        c.reduce(ctr[:, d * 520 + 512:d * 520 + 520], Et[:, d], op=ALU.mult)
                    xout = self.xchg(ctr[:], 1040, "gla")
                    self.combine_diag(H, xout, 1040, lambda d: d * 520, 512,
                                      lambda xr, d: xr[:, d * 520 + 512:d * 520 + 520].re("p (h o) -> p h o", o=1).bc([128, 8, 64]),
                                      lambda v: v.re("p (h v) -> p h v", h=8))
            self.gla_state(H, g_ktm, tm_i, g_E, g_S, store=True)
        with c.scope():
            nw = self.bcast_row("hgnw", inp["hg_norm_w"][j, :], 512)
            pr = c.ring("oP", 2, [128, 2, 8, 128], BF16)
            qr = c.ring("oq", 2, [128, 2, 8, 128], BF16)
            vr = c.ring("ov", 2, [128, 512], BF16)
            sr = c.ring("os", 2, [128, 2, 512], BF16)
            gr = c.ring("og", 2, [128, 512], BF16)
            sqr = c.ring("osq", 2, [128, 512], F32)
            yr = c.ring("oy", 2, [128, 512], F32)
            ybr = c.ring("oyb", 2, [128, 512], BF16)
            sgr = c.ring("osg", 2, [128, 512], F32)
            ssr = c.ring("oss", 2, [128, 8], F32)
            mtr = c.ring("omt", 2, [128, 4, 128], BF16)
            def chunk_body(t):
                col = slice(t * 128, (t + 1) * 128)
                P_ = pr.next(); q = qr.next(); v = vr.next(); S = sr.next(); gg = gr.next()
                for d in range(2):
                    c.dma("sp", P_[:, d], g_Pm[d, :, t, :, :].re("h p n -> p h n"))
                c.dma("sp", q[:], g_qin[:, :, :, col].re("d h p n -> p d h n"))
                c.dma("sp", v[:], tm_i[col, :])
                c.dma("sp", S[:], g_S[:, t, :, :].re("d p n -> p d n"))
                c.dma("sp", gg[:], tm_g[col, :])
                yield
                po = self.pb[4]
                for h in range(8):
                    oc = slice(h * 64, (h + 1) * 64)
                    c.mm(po[:, oc], P_[:, 0, h, :], v[:, oc], start=True, stop=False)
                    c.mm(po[:, oc], P_[:, 1, h, :], v[:, oc], start=False, stop=False)
                    c.mm(po[:, oc], q[:, 0, h, :], S[:, 0, oc], start=False, stop=False)
                    c.mm(po[:, oc], q[:, 1, h, :], S[:, 1, oc], start=False, stop=True)
                sq = sqr.next()
                c.act(sq[:], po[:, :], AF.Square)
                ss = ssr.next()
                c.reduce(ss[:], sq[:].re("p (h v) -> p h v", h=8))
                c.ts("dve", ss[:], ss[:], 1.0 / 64, ALU.mult, EPS, ALU.add)
                c.act(ss[:], ss[:], AF.Sqrt)
                c.recip(ss[:], ss[:])
                y = yr.next()
                c.tt("dve", y[:].re("p (h v) -> p h v", h=8), po[:, :].re("p (h v) -> p h v", h=8),
                     ss[:].re("p (h o) -> p h o", o=1).bc([128, 8, 64]), ALU.mult)
                c.tt("dve", y[:], y[:], nw[:], ALU.mult)
                sg = sgr.next()
                c.act(sg[:], gg[:], AF.Sigmoid)
                yb = ybr.next()
                c.tt("dve", yb[:], y[:], sg[:], ALU.mult)
                psb = self.pq[3].k(0)[:, 0:512].bitcast(BF16)
                for kc in range(4):
                    c.tr(psb[:, kc * 128:(kc + 1) * 128], yb[:, kc * 128:(kc + 1) * 128], self.cB("ident"))
                mt = mtr.next()
                c.copy("act", mt[:], psb[:, 0:512].re("p (a b) -> p a b", b=128))
                c.dma(SQ, self.mixT[0:512, col].re("(a p) n -> p a n", p=128), mt[:])
                yield
            self.pipeline((chunk_body(t) for t in range(NCH)), 1)

    def gla_pre_body(self, d, h, g, M4, R):
        c, TC, NT, NG, NCH = self.c, self.TC, self.NT, self.NG, self.NCH
        ncg = NT // 128
        (fr, qr, wk, qs, ks, qo, ko, eo, ktm, tmpP, pmo, omlb, lb, g_qin, g_ktm, g_Pm, g_E) = R
        col = slice(2 + g * NT, 2 + (g + 1) * NT)
        fl = fr.next()
        c.dma("sp", fl[:], self.uT[1024 + d * 1024 + h * 128:1024 + d * 1024 + (h + 1) * 128, col])
        q = qr.next()
        c.dma("sp", q[:], self.uT[h * 128:(h + 1) * 128, col])
        f = wk[0].next()
        c.act(f[:], fl[:], AF.Sigmoid)
        c.ts("dve", f[:], f[:], omlb[:, d, h:h + 1], ALU.mult, lb[:, d, h:h + 1], ALU.add)
        lg = wk[1].next()
        c.act(lg[:], f[:], AF.Ln)
        kk = wk[2].next()
        c.act(kk[:], f[:], AF.Identity, bias=1.0, scale=-1.0)
        yield
        b = wk[3].next()
        c.scan(b[:], self.rm128[:, 0:NT], lg[:], 0.0, ALU.mult, ALU.add)
        if d == 1:
            b3 = b[:].re("p (c l) -> p c l", l=128)
            c.tt("dve", lg[:], lg[:], b[:], ALU.subtract)
            c.tt("dve", b3, lg[:].re("p (c l) -> p c l", l=128), b3[:, :, 127:128].bc([128, ncg, 128]), ALU.add)
        Qs, Ks = [], []
        for li_, L in enumerate((16, 32, 64, 128)):
            nb = NT // L
            bL = b[:].re("p (c l) -> p c l", l=L)
            if L == 16:
                bi = 8
            else:
                bi = L // 2 - 1 if d == 0 else L // 2
            x = wk[4].next()
            c.tt("dve", x[:].re("p (c l) -> p c l", l=L), bL, bL[:, :, bi:bi + 1].bc([128, nb, L]), ALU.subtract)
            eq = wk[5].next()
            ek = f if li_ % 2 == 0 else lg
            if L == 16:
                c.ts("dve", x[:], x[:], -40.0, ALU.max, 40.0, ALU.min)
                c.act(eq[:], x[:], AF.Exp)
                c.act(ek[:], x[:], AF.Exp, scale=-1.0)
            else:
                c.act(eq[:], x[:], AF.Relu, scale=-1.0)
                c.act(eq[:], eq[:], AF.Exp, scale=-1.0)
                c.act(x[:], x[:], AF.Relu)
                c.act(ek[:], x[:], AF.Exp, scale=-1.0)
            qt = qs[li_].next(); kt = ks[li_].next()
            c.tt("dve", qt[:], q[:], eq[:], ALU.mult)
            c.tt("dve", kt[:], kk[:], ek[:], ALU.mult)
            Qs.append(qt); Ks.append(kt)
            yield
        yield
        pm = pmo.next()
        for tt in range(ncg):
            tc_ = slice(tt * 128, (tt + 1) * 128)
            ps = self.pb[tt % 4]
            for li_ in range(4):
                c.mm(ps[:, li_ * 128:(li_ + 1) * 128], Ks[li_][:, tc_], Qs[li_][:, tc_])
            tp = tmpP.next()
            c.tt("dve", tp[:], ps[:, :].re("p (l t) -> p l t", l=4), M4, ALU.mult)
            c.reduce(pm[:, tt, :], tp[:].re("p l t -> p t l"))
        c.dma(SQ, g_Pm[d, h, g * ncg:(g + 1) * ncg, :, :].re("a p n -> p a n"), pm[:])
        yield
        b3 = b[:].re("p (c l) -> p c l", l=128)
        endi = 127 if d == 0 else 0
        e = eo.next()
        c.act(e[:], b3[:, :, endi], AF.Exp)
        c.dma(SQ, g_E[d, h, :, g * ncg:(g + 1) * ncg], e[:])
        x = wk[4].next()
        c.tt("dve", x[:].re("p (c l) -> p c l", l=128), b3, b3[:, :, endi:endi + 1].bc([128, ncg, 128]), ALU.subtract)
        ee = wk[5].next()
        c.act(ee[:], x[:], AF.Exp, scale=-1.0)
        kt = ko.next()
        c.tt("dve", kt[:], kk[:], ee[:], ALU.mult)
        eb = f
        c.act(eb[:], b[:], AF.Exp)
        qt = qo.next()
        c.tt("dve", qt[:], q[:], eb[:], ALU.mult)
        c.dma(SQ, g_qin[d, h, :, g * NT:(g + 1) * NT], qt[:])
        kT = ktm.next()
        psb = self.pq[3].k(0)[:, 0:512].bitcast(BF16)
        for tt in range(ncg):
            c.tr(psb[:, tt * 128:(tt + 1) * 128], kt[:, tt * 128:(tt + 1) * 128], self.cB("ident"))
        c.copy("act", kT[:], psb[:, 0:NT].re("p (a b) -> p a b", b=128))
        c.dma(SQ, g_ktm[g * NT:(g + 1) * NT, (d * 8 + h) * 128:(d * 8 + h + 1) * 128].re("(a p) n -> p a n", p=128), kT[:])
        yield

    def gla_state(self, H2, g_ktm, tm_i, g_E, g_S, store):
        c, NCH = self.c, self.NCH
        with c.scope():
            Et = c.sb("gEt", [128, 2, 8, NCH], F32)
            c.dma("sp", Et[:], g_E[:].re("d h p n -> p d h n"))
            H = [H2[d][:].re("p (h v) -> p h v", h=8) for d in range(2)]
            tmp = c.ring("gtmp", 2, [128, 8, 64], F32)
            hb = c.ring("gHb", 4, [128, 512], BF16)
            kr = c.ring("skt", 3, [128, 1024], BF16)
            vr = c.ring("svt", 3, [128, 512], BF16)
            def dir_body(d):
              order = range(NCH) if d == 0 else range(NCH - 1, -1, -1)
              for t in order:
                if True:
                    kt = kr.next(); vt = vr.next()
                    c.dma("sp", kt[:], g_ktm[t * 128:(t + 1) * 128, d * 1024:(d + 1) * 1024])
                    c.dma("sp", vt[:], tm_i[t * 128:(t + 1) * 128, :])
                    if store:
                        b = hb.next()
                        c.copy("act", b[:], H2[d][:])
                        c.dma(SQ, g_S[d, t, :, :], b[:])
                    ps = self.pb[d]
                    for h in range(8):
                        c.mm(ps[:, h * 64:(h + 1) * 64], kt[:, h * 128:(h + 1) * 128], vt[:, h * 64:(h + 1) * 64])
                    tp = tmp.next()
                    yield
                    c.tt("dve", tp[:], H[d], Et[:, d, :, t:t + 1].bc([128, 8, 64]), ALU.mult)
                    c.tt("dve", H[d], tp[:], ps[:, :].re("p (h v) -> p h v", h=8), ALU.add)
                    yield
            self.pipeline([dir_body(0), dir_body(1)], 2)


class Prog(Prog):
    def coef_tile(self):
        return self._coef

    def xchg(self, tile, W, tag):
        c = self.c
        outs = []
        for k, c0 in enumerate(range(0, W, 1024)):
            w = min(1024, W - c0)
            xin = self.scratch(f"xin_{tag}{k}", [128, w], F32)
            xout = self.scratch(f"xout_{tag}{k}", [self.NC * 128, w], F32)
            c.dma("pool", xin[:, :], tile[:, c0:c0 + w])
            c.allgather(xout[:, :], xin[:, :], self.NC)
            outs.append((xout, c0, w))
        return outs

    def load_rank(self, xr, outs, r):
        for xout, c0, w in outs:
            self.c.dma("sp", xr[:, c0:c0 + w], xout[r * 128:(r + 1) * 128, :])

    def combine_diag(self, H, xout, W, offN, sizeN, Mfn, view):
        c = self.c
        co = self.coef_tile()
        xr_ring = c.ring("xr", 2, [128, W], F32)
        t1r = c.ring("xt1", 2, [128, sizeN], F32)
        for d in range(2):
            c.memset("pool", H[d][:], 0.0)
        for step in range(self.NC):
            for d in range(2):
                r = step if d == 0 else self.NC - 1 - step
                xr = xr_ring.next()
                self.load_rank(xr, xout, r)
                a = co[:, 16 + r:17 + r] if d == 0 else co[:, 40 + r:41 + r]
                b = co[:, 24 + r:25 + r] if d == 0 else co[:, 32 + r:33 + r]
                t1 = t1r.next()
                Hv = view(H[d][:])
                t1v = view(t1[:])
                Nv = view(xr[:, offN(d):offN(d) + sizeN])
                c.tt("dve", t1v, Hv, Mfn(xr, d), ALU.mult)
                c.tt("dve", t1v, t1v, Nv, ALU.add)
                c.ts("dve", t1[:], t1[:], a, ALU.mult)
                c.stt(Hv, Hv, b, t1v, ALU.mult, ALU.add)

    def halo_prepare(self, li):
        c, TC = self.c, self.TC
        co = self.coef_tile()
        hb = c.sb("hbnd", [128, 64], F32)
        c.memset("pool", hb[:], 0.0)
        hv = hb[:, 0:24].re("p (k n) -> p k n", n=3)
        c.dma("sp", hv[:, :, 0:1], self.hT[:, 0:1].re("(k p) n -> p k n", p=128), nonc=True)
        c.dma("sp", hv[:, :, 1:3], self.hT[:, TC - 2:TC].re("(k p) n -> p k n", p=128), nonc=True)
        xout = self.xchg(hb[:], 64, "halo")
        G = c.sb("hG", [128, self.NC, 24], F32)
        c.dma("sp", G[:], xout[0][0][:, 0:24].re("(r p) n -> p r n", p=128))
        hh = c.sb("hh", [128, 8, 3], F32)
        c.memset("pool", hh[:], 0.0)
        for r in range(self.NC):
            Gv = G[:, r, :].re("p (k n) -> p k n", n=3)
            c.stt(hh[:, :, 0:2], Gv[:, :, 1:3], co[:, r:r + 1], hh[:, :, 0:2], ALU.mult, ALU.add)
            c.stt(hh[:, :, 2:3], Gv[:, :, 0:1], co[:, 8 + r:9 + r], hh[:, :, 2:3], ALU.mult, ALU.add)
        gam = self.load_gamma(self.inp["ln_mix"][li, :])
        sq = c.sb("hsq", [128, 8, 4], BF16)
        c.memset("pool", sq[:], 0.0)
        c.act(sq[:, :, 0:3], hh[:], AF.Square)
        ps = self.pb[7]
        for k in range(8):
            c.mm(ps[:, 0:4], self.cB("ones"), sq[:, k, :], start=(k == 0), stop=(k == 7))
        rs = c.sb("hrs", [128, 4], F32)
        c.ts("dve", rs[:], ps[:, 0:4], 1.0 / D, ALU.mult, EPS, ALU.add)
        c.act(rs[:], rs[:], AF.Sqrt)
        c.recip(rs[:], rs[:])
        hn = c.sb("hhn", [128, 8, 4], BF16)
        c.memset("pool", hn[:], 0.0)
        for k in range(8):
            c.stt(hn[:, k, 0:3], hh[:, k, :], gam[:, k:k + 1], rs[:, 0:3], ALU.mult, ALU.mult)
        return hn

    def proj_halo(self, wb, sub, m, r0):
        c, TC = self.c, self.TC
        hn = self.halo_hn
        ps = self.pb[6]
        for k in range(8):
            c.mm(ps[0:m, 0:4], wb[:, k, sub:sub + m], hn[:, k, :], start=(k == 0), stop=(k == 7))
        st = self.st_ring.next()
        c.copy("act", st[0:m, 0:4], ps[0:m, 0:4])
        c.dma(SQ, self.uT[r0:r0 + m, 0:2], st[0:m, 0:2], nonc=True)
        c.dma(SQ, self.uT[r0:r0 + m, TC + 2:TC + 3], st[0:m, 2:3], nonc=True)
```
